# Optimizing a Trainium2 kernel written in Bass

```python
import jax, jax.numpy as jnp
from jax import lax
import numpy as np

D_MODEL = 2048
BATCH = 1
SEQ = 16384
DEPTH = 2
DEC_BATCH = 2
DEC_SEQ = 8192
PAST_LEN = 128

RET_HEADS = 8
RET_DK = D_MODEL // RET_HEADS
RET_DV = 2 * D_MODEL // RET_HEADS
RET_CHUNK = 128
ROPE_BASE = 10000.0
SGU_E = 3 * D_MODEL
SGU_GROUPS = 8
SGU_CHUNK = 128
FFN_DIM = 5632
CONV_W = 3
N_RET = (DEPTH + 1) // 2
N_SGU = DEPTH // 2
EPS = 1e-6
LN_EPS = 1e-5

kernel_name = "bidir_retention_gmlp_convffn_hybrid"


def rms_norm(x, g):
    x32 = x.astype(jnp.float32)
    y = x32 * lax.rsqrt(jnp.mean(x32 * x32, axis=-1, keepdims=True) + EPS)
    return (y * g.astype(jnp.float32)).astype(x.dtype)


def rope(x):
    S, d = x.shape[1], x.shape[-1]
    half = d // 2
    inv = ROPE_BASE ** (-jnp.arange(half, dtype=jnp.float32) / half)
    ang = jnp.arange(S, dtype=jnp.float32)[:, None] * inv[None, :]
    cos = jnp.cos(ang)[None, :, None, :].astype(x.dtype)
    sin = jnp.sin(ang)[None, :, None, :].astype(x.dtype)
    x1, x2 = x[..., :half], x[..., half:]
    return jnp.concatenate([x1 * cos - x2 * sin, x2 * cos + x1 * sin], axis=-1)


def retention_one_direction(q, k, v, log_gamma, include_diag):
    B, S, H, dk = q.shape
    dv = v.shape[-1]
    n_chunks = S // RET_CHUNK

    def to_chunks(t):
        return t.reshape(B, n_chunks, RET_CHUNK, H, t.shape[-1]).transpose(1, 0, 3, 2, 4)

    qc, kc, vc = to_chunks(q), to_chunks(k), to_chunks(v)
    lg = log_gamma.astype(jnp.float32)
    pos = jnp.arange(RET_CHUNK, dtype=jnp.float32)
    diff = pos[:, None] - pos[None, :]
    mask = (diff >= 0) if include_diag else (diff > 0)
    decay_intra = jnp.where(mask, jnp.exp(lg[:, None, None] * jnp.maximum(diff, 0.0)), 0.0)
    q_decay = jnp.exp(lg[:, None] * (pos + 1.0))
    k_decay = jnp.exp(lg[:, None] * (RET_CHUNK - 1.0 - pos))
    chunk_decay = jnp.exp(lg * RET_CHUNK)

    def step(state, inp):
        qi, ki, vi = inp
        qi = qi.astype(jnp.float32)
        ki = ki.astype(jnp.float32)
        vi = vi.astype(jnp.float32)
        scores = jnp.einsum('bhid,bhjd->bhij', qi, ki) * decay_intra
        intra = jnp.einsum('bhij,bhjv->bhiv', scores, vi)
        cross = jnp.einsum('bhid,bhdv->bhiv', qi * q_decay[:, :, None], state)
        new_state = state * chunk_decay[:, None, None] + jnp.einsum(
            'bhjd,bhjv->bhdv', ki * k_decay[:, :, None], vi)
        return new_state, (intra + cross).astype(v.dtype)

    state0 = jnp.zeros((B, H, dk, dv), jnp.float32)
    _, out = lax.scan(step, state0, (qc, kc, vc))
    return out.transpose(1, 0, 3, 2, 4).reshape(B, S, H, dv)


def retention_mixer(h, w_in, log_decay_fwd, log_decay_bwd, gn_gain, w_out):
    B, S, _ = h.shape
    hq = RET_HEADS * RET_DK
    hv = RET_HEADS * RET_DV
    proj = h @ w_in
    q, k, v, g = jnp.split(proj, [hq, 2 * hq, 2 * hq + hv], axis=-1)
    q = rope(q.reshape(B, S, RET_HEADS, RET_DK))
    k = rope(k.reshape(B, S, RET_HEADS, RET_DK)) * (RET_DK ** -0.5)
    v = v.reshape(B, S, RET_HEADS, RET_DV)
    y_fwd = retention_one_direction(q, k, v, log_decay_fwd, True)
    flip = lambda t: jnp.flip(t, axis=1)
    y_bwd = flip(retention_one_direction(flip(q), flip(k), flip(v), log_decay_bwd, False))
    y = (y_fwd + y_bwd).astype(jnp.float32)
    mu = jnp.mean(y, axis=-1, keepdims=True)
    var = jnp.mean(jnp.square(y - mu), axis=-1, keepdims=True)
    y = ((y - mu) * lax.rsqrt(var + LN_EPS)).reshape(B, S, hv) * gn_gain.astype(jnp.float32)
    return (jax.nn.silu(g) * y.astype(h.dtype)) @ w_out


def sgu_mixer(h, w_in, b_in, ln_g, ln_b, w_s, b_s, w_out):
    B, S, _ = h.shape
    z = jax.nn.gelu(h @ w_in + b_in, approximate=False)
    u, v = jnp.split(z, 2, axis=-1)
    v32 = v.astype(jnp.float32)
    mu = jnp.mean(v32, axis=-1, keepdims=True)
    var = jnp.mean(jnp.square(v32 - mu), axis=-1, keepdims=True)
    vn = ((v32 - mu) * lax.rsqrt(var + LN_EPS) * ln_g.astype(jnp.float32)
          + ln_b.astype(jnp.float32)).astype(h.dtype)
    n_chunks = S // SGU_CHUNK
    vg = vn.reshape(B, n_chunks, SGU_CHUNK, SGU_GROUPS, SGU_E // SGU_GROUPS)
    mixed = jnp.einsum('gij,bnjgc->bnigc', w_s, vg) + b_s.T[:, :, None]
    return (u * mixed.reshape(B, S, SGU_E)) @ w_out


def conv_ffn(h, w_in, conv_w, conv_b, w_out):
    a = h @ w_in
    ap = jnp.pad(a, ((0, 0), (1, 1), (0, 0)))
    a = ap[:, :-2] * conv_w[0] + ap[:, 1:-1] * conv_w[1] + ap[:, 2:] * conv_w[2] + conv_b
    gate, up = jnp.split(a, 2, axis=-1)
    return (jax.nn.silu(gate) * up) @ w_out


def trunk(x, ln_mix, ln_ffn, ret_w_in, ret_log_decay_fwd, ret_log_decay_bwd, ret_gn_gain,
          ret_w_out, sgu_w_in, sgu_b_in, sgu_ln_g, sgu_ln_b, sgu_w_s, sgu_b_s, sgu_w_out,
          ffn_w_in, ffn_conv_w, ffn_conv_b, ffn_w_out, ln_final):
    for i in range(DEPTH):
        j = i // 2
        hn = rms_norm(x, ln_mix[i])
        if i % 2 == 0:
            x = x + retention_mixer(hn, ret_w_in[j], ret_log_decay_fwd[j], ret_log_decay_bwd[j],
                                    ret_gn_gain[j], ret_w_out[j])
        else:
            x = x + sgu_mixer(hn, sgu_w_in[j], sgu_b_in[j], sgu_ln_g[j], sgu_ln_b[j],
                              sgu_w_s[j], sgu_b_s[j], sgu_w_out[j])
        hn = rms_norm(x, ln_ffn[i])
        x = x + conv_ffn(hn, ffn_w_in[i], ffn_conv_w[i], ffn_conv_b[i], ffn_w_out[i])
    return rms_norm(x, ln_final)


def setup_inputs(seed: int = 0) -> dict:
    key = jax.random.key(seed)
    ks = jax.random.split(key, 24)
    f32 = jnp.float32
    nrm = lambda k, shape, s: jax.random.normal(k, shape, f32) * s
    hq = RET_HEADS * RET_DK
    hv = RET_HEADS * RET_DV
    base_decay = jnp.asarray(np.log(1.0 - 2.0 ** (-5.0 - np.arange(RET_HEADS))), f32)
    conv_center = jnp.asarray([0.0, 1.0, 0.0], f32)[None, :, None]
    return {
        "x_prompt": jax.random.normal(ks[0], (BATCH, SEQ, D_MODEL), f32),
        "x_sample": jax.random.normal(ks[1], (DEC_BATCH, DEC_SEQ, D_MODEL), f32),
        "ln_mix": 1.0 + nrm(ks[2], (DEPTH, D_MODEL), 0.02),
        "ln_ffn": 1.0 + nrm(ks[3], (DEPTH, D_MODEL), 0.02),
        "ret_w_in": nrm(ks[4], (N_RET, D_MODEL, 2 * hq + 2 * hv), D_MODEL ** -0.5),
        "ret_log_decay_fwd": base_decay[None, :] * jnp.exp(nrm(ks[5], (N_RET, RET_HEADS), 0.1)),
        "ret_log_decay_bwd": base_decay[None, :] * jnp.exp(nrm(ks[6], (N_RET, RET_HEADS), 0.1)),
        "ret_gn_gain": 1.0 + nrm(ks[7], (N_RET, hv), 0.02),
        "ret_w_out": nrm(ks[8], (N_RET, hv, D_MODEL), hv ** -0.5),
        "sgu_w_in": nrm(ks[9], (N_SGU, D_MODEL, 2 * SGU_E), D_MODEL ** -0.5),
        "sgu_b_in": nrm(ks[10], (N_SGU, 2 * SGU_E), 0.02),
        "sgu_ln_g": 1.0 + nrm(ks[11], (N_SGU, SGU_E), 0.02),
        "sgu_ln_b": nrm(ks[12], (N_SGU, SGU_E), 0.02),
        "sgu_w_s": nrm(ks[13], (N_SGU, SGU_GROUPS, SGU_CHUNK, SGU_CHUNK), SGU_CHUNK ** -0.5),
        "sgu_b_s": 1.0 + nrm(ks[14], (N_SGU, SGU_GROUPS, SGU_CHUNK), 0.02),
        "sgu_w_out": nrm(ks[15], (N_SGU, SGU_E, D_MODEL), SGU_E ** -0.5),
        "ffn_w_in": nrm(ks[16], (DEPTH, D_MODEL, 2 * FFN_DIM), D_MODEL ** -0.5),
        "ffn_conv_w": conv_center + nrm(ks[17], (DEPTH, CONV_W, 2 * FFN_DIM), 0.2),
        "ffn_conv_b": nrm(ks[18], (DEPTH, 2 * FFN_DIM), 0.02),
        "ffn_w_out": nrm(ks[19], (DEPTH, FFN_DIM, D_MODEL), FFN_DIM ** -0.5),
        "ln_final": 1.0 + nrm(ks[20], (D_MODEL,), 0.02),
    }


def reference(x_prompt, x_sample, ln_mix, ln_ffn, ret_w_in, ret_log_decay_fwd, ret_log_decay_bwd,
              ret_gn_gain, ret_w_out, sgu_w_in, sgu_b_in, sgu_ln_g, sgu_ln_b, sgu_w_s, sgu_b_s,
              sgu_w_out, ffn_w_in, ffn_conv_w, ffn_conv_b, ffn_w_out, ln_final):
    params = (ln_mix, ln_ffn, ret_w_in, ret_log_decay_fwd, ret_log_decay_bwd, ret_gn_gain,
              ret_w_out, sgu_w_in, sgu_b_in, sgu_ln_g, sgu_ln_b, sgu_w_s, sgu_b_s, sgu_w_out,
              ffn_w_in, ffn_conv_w, ffn_conv_b, ffn_w_out, ln_final)
    y_prompt = trunk(x_prompt, *params)
    y_sample = trunk(x_sample, *params)
    return (y_prompt, y_sample)
```

```python
import math
import os
_SKIP = os.environ.get('KSKIP', '').split(',')
import numpy as np
import ml_dtypes
import concourse.bass as bass
import concourse.mybir as mybir
from concourse.bass_utils import run_bass_kernel_spmd

F32 = mybir.dt.float32
BF16 = mybir.dt.bfloat16
AF = mybir.ActivationFunctionType
ALU = mybir.AluOpType

D = 2048
H = 8
DK = 256
DV = 512
HQ = H * DK
HV = H * DV
E = 3 * D
G = 8
FF = 5632
EPS = 1e-6
LN_EPS = 1e-5
T = 512
N_CORES = 8

ENGS = ("pe", "act", "dve", "pool", "sp")
EPOCH = 12000


class Op:
    __slots__ = ("idx", "eng", "fn", "deps", "dma", "pos", "waits", "signal", "event")

    def __init__(self, idx, eng, fn, deps, dma, pos):
        self.idx = idx
        self.eng = eng
        self.fn = fn
        self.deps = deps
        self.dma = dma
        self.pos = pos
        self.waits = []
        self.signal = False
        self.event = None


class Prog:
    def __init__(self):
        self.ops = []
        self.eng_ops = {e: [] for e in ENGS}
        self.last_w = {}
        self.readers = {}
        self.dma_since = []

    def add(self, eng, fn, r=(), w=(), dma=None, extra=()):
        idx = len(self.ops)
        deps = set(extra)
        if dma is not None:
            self.dma_since.append(idx)
        for k in r:
            lw = self.last_w.get(k)
            if lw is not None:
                deps.add(lw)
        for k in w:
            lw = self.last_w.get(k)
            if lw is not None:
                deps.add(lw)
            rd = self.readers.get(k)
            if rd:
                deps.update(rd.values() if isinstance(rd, dict) else rd)
        op = Op(idx, eng, fn, sorted(deps), dma, len(self.eng_ops[eng]))
        self.ops.append(op)
        self.eng_ops[eng].append(op)
        for k in r:
            rd = self.readers.setdefault(k, {})
            key = (eng, idx) if dma is not None else eng
            rd[key] = idx
        for k in w:
            self.last_w[k] = idx
            self.readers[k] = {}
        return idx

    def barrier(self):
        pend = list(self.dma_since)
        self.dma_since = []
        last = []
        for e in ENGS:
            for op in reversed(self.eng_ops[e]):
                if op.fn is not None and op.dma is None:
                    last.append(op.idx)
                    break
        for e in ENGS:
            self.add(e, None, extra=last + pend)
        self.last_w = {}
        self.readers = {}

    def finalize(self, nc, sem_alloc):
        ops = self.ops
        waited = {e: {f: -1 for f in ENGS} for e in ENGS}
        waited_dma = {e: {} for e in ENGS}
        dma_count = {}
        for op in ops:
            if op.dma is not None:
                n = dma_count.get(op.dma, 0)
                dma_count[op.dma] = n + 1
                ep, k = divmod(n, 1000)
                op.event = ((op.dma, ep), (k + 1) * 16)
        for op in ops:
            e = op.eng
            best = {}
            for d in op.deps:
                y = ops[d]
                if y.dma is not None:
                    if y.event[0] not in best or ops[best[y.event[0]]].event[1] < y.event[1]:
                        best[y.event[0]] = d
            for d in op.deps:
                y = ops[d]
                if y.dma is not None and best[y.event[0]] != d:
                    continue
                if y.dma is not None:
                    sem, val = y.event
                    if waited_dma[e].get(sem, 0) >= val:
                        continue
                    waited_dma[e][sem] = val
                    op.waits.append(d)
                    continue
                if y.fn is None and y.eng == e:
                    continue
                if y.eng == e:
                    if e == "pe":
                        continue
                    if op.pos - y.pos > 2:
                        continue
                if waited[e][y.eng] >= y.pos:
                    continue
                waited[e][y.eng] = y.pos
                op.waits.append(d)
                y.signal = True
        self.sems = {}
        for e in ENGS:
            cnt = 0
            for op in self.eng_ops[e]:
                if op.dma is None and op.signal:
                    ep, v = divmod(cnt, EPOCH)
                    op.event = (("eng", e, ep), v + 1)
                    cnt += 1
        keys = set()
        for op in ops:
            if op.event is not None and (op.dma is not None or op.signal):
                keys.add(op.event[0])
        for k in sorted(keys, key=str):
            self.sems[k] = sem_alloc(k)

    def emit(self, eng_name, eng):
        ops = self.ops
        for op in self.eng_ops[eng_name]:
            for d in op.waits:
                y = ops[d]
                eng.wait_ge(self.sems[y.event[0]], y.event[1])
            if op.fn is None:
                if op.signal:
                    eng.sem_inc(self.sems[op.event[0]], 1)
                continue
            ins = op.fn(eng)
            if op.dma is not None:
                ins.then_inc(self.sems[op.event[0]], 16)
            elif op.signal:
                ins.then_inc(self.sems[op.event[0]], 1)


def _blocks(start, n, bs=128):
    out = []
    o = 0
    while o < n:
        b = min(bs, n - o)
        out.append((start + o, b, o))
        o += b
    return out


def ffn_tiles(nt):
    n = max(1, math.ceil(nt / 456))
    base = nt // n
    rem = nt - base * n
    tiles = []
    s = 0
    for i in range(n):
        sz = base + (1 if i < rem else 0)
        tiles.append((s, sz))
        s += sz
    return tiles


def make_consts():
    c = {}
    p = np.arange(128, dtype=np.float32)
    i = np.arange(128, dtype=np.float32)
    diff = i[None, :] - p[:, None]
    cst = np.zeros((128, 1024), np.float32)
    cst[:, 0:128] = np.maximum(diff, 0.0)
    cst[:, 128:256] = np.maximum(-diff, 0.0)
    cst[:, 256:384] = (diff >= 0) / 16.0
    cst[:, 384:512] = (diff < 0) / 16.0
    cst[:, 512:640] = (i + 1.0)[None, :]
    cst[:, 640:768] = (128.0 - i)[None, :]
    for tb in range(4):
        cst[:, 768 + tb] = 511.0 - (tb * 128 + p)
        cst[:, 772 + tb] = tb * 128 + p
    cst[:, 776] = 127.0 - p
    cst[:, 777] = p
    c["cst"] = cst
    ident = np.eye(128, dtype=np.float32)
    c["identb"] = ident.astype(ml_dtypes.bfloat16)
    sel = np.zeros((24, 24, 128), np.float32)
    for r in range(24):
        sel[r, r, :] = 1.0
    c["sel"] = sel.astype(ml_dtypes.bfloat16)
    return c


def rope_tables(pos):
    half = DK // 2
    inv = (10000.0 ** (-(np.arange(half, dtype=np.float32) / np.float32(half)))).astype(np.float32)
    ang = (np.asarray(pos, dtype=np.float32)[None, :] * inv[:, None]).astype(np.float32)
    return np.cos(ang).astype(np.float32), np.sin(ang).astype(np.float32)


EXT = 256


def build(NOWN, NF, stop_after="all"):
    NT = NOWN + 2 * EXT
    assert NT % T == 0
    NTILE = NT // T
    nc = bass.Bass("TRN2", target_bir_lowering=False)
    P = Prog()

    def din(name, shape, dt=F32):
        return nc.dram_tensor(name, list(shape), dt, kind="ExternalInput").ap()

    def dscr(name, shape, dt=F32):
        return nc.dram_tensor(name, list(shape), dt, kind="Internal").ap()

    x_in = din("x", [NT, D])
    tokmask = din("tokmask", [NT + 2, 1])
    ln_mix = din("ln_mix", [2, D])
    ln_ffn = din("ln_ffn", [2, D])
    ret_w_in = din("ret_w_in", [D, 2 * HQ + 2 * HV])
    lgf = din("lgf", [1, H])
    lgb = din("lgb", [1, H])
    gn_gain = din("gn_gain", [1, HV])
    ret_w_out = din("ret_w_out", [HV, D])
    sgu_w_in = din("sgu_w_in", [D, 2 * E])
    sgu_b_in = din("sgu_b_in", [1, 2 * E])
    sgu_ln_g = din("sgu_ln_g", [1, E])
    sgu_ln_b = din("sgu_ln_b", [1, E])
    sgu_w_s = din("sgu_w_s", [G, 128, 128])
    sgu_b_s = din("sgu_b_s", [G, 128])
    sgu_w_out = din("sgu_w_out", [E, D])
    ffn_w_in = [din("ffn_w_in%d" % i, [D, 2 * FF]) for i in range(2)]
    ffn_conv_w = [din("ffn_conv_w%d" % i, [3, 2 * FF]) for i in range(2)]
    ffn_conv_b = [din("ffn_conv_b%d" % i, [1, 2 * FF]) for i in range(2)]
    ffn_w_out = [din("ffn_w_out%d" % i, [FF, D]) for i in range(2)]
    ln_final = din("ln_final", [1, D])
    cst_d = din("cst", [128, 1024])
    identb_d = din("identb", [128, 128], BF16)
    sel_d = din("sel", [24, 24, 128], BF16)
    cos_d = din("cos", [128, NT])
    sin_d = din("sin", [128, NT])
    xf_in = din("xf", [NF * T, D])
    cosf_d = din("cosf", [128, NF * T])
    sinf_d = din("sinf", [128, NF * T])
    ftab_d = din("ftab", [128, NF * 12])
    y_out = nc.dram_tensor("y", [NOWN, D], F32, kind="ExternalOutput").ap()
    sinb_s = dscr("sinb_s", [128, 16, 512])

    wb_ret_in = dscr("wb_ret_in", [D, 2 * HQ + 2 * HV], BF16)
    wb_ret_out = dscr("wb_ret_out", [HV, D], BF16)
    wb_sgu_in = dscr("wb_sgu_in", [D, 2 * E], BF16)
    wb_sgu_out = dscr("wb_sgu_out", [E, D], BF16)
    wb_ffn_in = [dscr("wb_ffn_in%d" % i, [D, 2 * FF], BF16) for i in range(2)]
    wb_ffn_out = [dscr("wb_ffn_out%d" % i, [FF, D], BF16) for i in range(2)]
    xs = [dscr("xs%d" % i, [NT + 2, D]) for i in range(4)]
    qT_s = dscr("qT_s", [H, 2, 128, NT], BF16)
    kT_s = dscr("kT_s", [H, 2, 128, NT], BF16)
    v_s = dscr("v_s", [H, NT, DV], BF16)
    sg_s = dscr("sg_s", [H, NT, DV], BF16)
    F_s = dscr("F_s", [NTILE, 128, 16, 512])
    B_s = dscr("B_s", [NTILE, 128, 16, 512])
    Sf_s = dscr("Sf_s", [NTILE, 128, 16, 512], BF16)
    Sb_s = dscr("Sb_s", [NTILE, 128, 16, 512], BF16)

    import contextlib
    es = contextlib.ExitStack()

    def sb(name, shape, dt=F32):
        return es.enter_context(nc.sbuf_tensor(name, list(shape), dt))

    def pst(name, shape, dt=F32):
        return es.enter_context(nc.psum_tensor(name, list(shape), dt))

    W = [sb("w%d" % i, [128, 8, 512], BF16) for i in range(4)]
    HN = sb("hn", [128, 16, 516], BF16)
    XIN = [sb("xin%d" % i, [128, D]) for i in range(2)]
    XN = sb("xn", [128, D], BF16)
    RES = [sb("res0", [128, 4, 512])] * 2
    BIG1 = sb("big1", [128, 44 * 512], BF16)
    BIG2 = sb("big2", [128, 36 * 512], BF16)
    PC = sb("phaseconst", [128, 7168])
    GT = sb("gt", [128, D])
    TMP = [sb("tmp%d" % i, [128, 512]) for i in range(4)]
    CST = BIG1[:, 8192:10240].bitcast(F32)
    IDB = sb("identb_sb", [128, 128], BF16)
    LG = sb("lg", [128, 16])
    SMALL = sb("small", [128, 64])
    EPSC = sb("epsc", [128, 2])
    STAT = sb("stat", [128, 4 * 12 * 6])
    DEC = sb("dec", [128, 80])
    QD = PC[:, 0:2048].rearrange("p (a n) -> p a n", a=16)
    DM = PC[:, 2048:3072].rearrange("p (a n) -> p a n", a=8)
    CW = sb("convw", [128, 88 * 4])
    PS = [pst("ps%d" % i, [128, 512]) for i in range(6)]
    PT = [pst("pt%d" % i, [128, 1024], BF16) for i in range(2)]

    sem_stack = []

    def sem_alloc(key):
        s = es.enter_context(nc.semaphore("s%d" % len(sem_stack)))
        sem_stack.append(s)
        return s

    def dma(eng, out, in_, r, w, sem, slow=False):
        if 'st' in _SKIP and eng == "pool" and sem not in ("cast", "zp") and not sem.startswith("yst"):
            return
        if slow:
            P.add(eng, lambda e, out=out, in_=in_: e.dma_start(out=out, in_=in_, allow_slow_non_contiguous=True), r=r, w=w, dma=sem)
        else:
            P.add(eng, lambda e, out=out, in_=in_: e.dma_start(out=out, in_=in_), r=r, w=w, dma=sem)

    out_stores = []

    cast_pending = []

    def cast_w(dst, src, key, now=False):
        if 'cast' in _SKIP:
            return
        R, C = src.shape
        for r0 in range(0, R, 2048):
            r1 = min(R, r0 + 2048)
            for c0 in range(0, C, 2048):
                c1 = min(C, c0 + 2048)
                cast_pending.append((dst[r0:r1, c0:c1], src[r0:r1, c0:c1], key))
        if now:
            emit_casts(len(cast_pending))

    def emit_casts(n):
        for _ in range(min(n, len(cast_pending))):
            d_, s_, key = cast_pending.pop(0)
            dma("pool", d_, s_, r=["castchain"], w=["castchain", key], sem="cast")

    wstate = {"n": 0}

    class WT:
        def __init__(self, slots):
            self.slots = slots
            self.keys = ["W%d" % sl for sl in slots]

        def ap(self, kc, c0, c1):
            return W[self.slots[kc // 8]][:, kc % 8, c0:c1]

    def load_w(pieces, key):
        kcmax = max(p[0] for p in pieces)
        nsl = (kcmax + 7) // 8
        slots = []
        for g in range(nsl):
            sl = wstate["n"] % 4
            wstate["n"] += 1
            slots.append(sl)
            for (kc, c0, ncol, ap) in pieces:
                k0, k1 = g * 8, min(kc, g * 8 + 8)
                if k1 > k0:
                    dma("sp", W[sl][:, 0:k1 - k0, c0:c0 + ncol], ap[:, k0:k1, :], r=[key], w=["W%d" % sl], sem="w%d" % sl)
        return WT(slots)

    def wtile(wb, k0, kc, c0, ncol):
        return wb[k0 * 128:(k0 + kc) * 128, c0:c0 + ncol].rearrange("(kc p) n -> p kc n", p=128)

    dma("sp", CST[:], cst_d[:, :], r=(), w=["CST"], sem="cst")
    dma("sp", IDB[:], identb_d[:, :], r=(), w=["IDB"], sem="idb")
    dma("sp", LG[:, 0:8], lgf[0:1, :].partition_broadcast(128), r=(), w=["LG"], sem="lg")
    dma("sp", LG[:, 8:16], lgb[0:1, :].partition_broadcast(128), r=(), w=["LG"], sem="lg")
    P.add("dve", lambda e: e.memset(EPSC[:, 0:1], EPS), w=["EPSC"])
    P.add("dve", lambda e: e.memset(EPSC[:, 1:2], LN_EPS), r=["EPSC"], w=["EPSC"])
    P.add("dve", lambda e: e.memset(TMP[0][:], 0.0), w=["TMP0"])
    for i in range(4 if 'zp' not in _SKIP else 0):
        dma("pool", xs[i][0:1, :].rearrange("a (b c) -> (a b) c", b=4), TMP[0][0:4, :], r=["TMP0"], w=["xs%d" % i], sem="zp")
        dma("pool", xs[i][NT + 1:NT + 2, :].rearrange("a (b c) -> (a b) c", b=4), TMP[0][0:4, :], r=["TMP0"], w=["xs%d" % i], sem="zp")

    cast_w(wb_ret_in, ret_w_in, "wb_ret_in", now=True)
    cast_w(wb_ret_out, ret_w_out, "wb_ret_out")
    cast_w(wb_ffn_in[0], ffn_w_in[0], "wb_ffn_in0")
    cast_w(wb_ffn_out[0], ffn_w_out[0], "wb_ffn_out0")
    cast_w(wb_sgu_in, sgu_w_in, "wb_sgu_in")
    cast_w(wb_sgu_out, sgu_w_out, "wb_sgu_out")
    cast_w(wb_ffn_in[1], ffn_w_in[1], "wb_ffn_in1")
    cast_w(wb_ffn_out[1], ffn_w_out[1], "wb_ffn_out1")

    def exp_tab(out_ap, in_ap, scale_ap, post=None):
        P.add("act", lambda e: e.activation(out=out_ap, in_=in_ap, func=AF.Exp, scale=scale_ap),
              r=["CST", "LG"], w=["DEC"])
    for tb in range(4 if 'tab' not in _SKIP else 0):
        exp_tab(DEC[:, tb * 8:tb * 8 + 8], LG[:, 0:8], CST[:, 768 + tb:769 + tb])
        exp_tab(DEC[:, 32 + tb * 8:32 + tb * 8 + 8], LG[:, 8:16], CST[:, 772 + tb:773 + tb])
    exp_tab(DEC[:, 64:72], LG[:, 0:8], CST[:, 776:777])
    exp_tab(DEC[:, 72:80], LG[:, 8:16], CST[:, 777:778])
    P.add("dve", lambda e: e.tensor_scalar(out=DEC[:, 0:80], in0=DEC[:, 0:80], scalar1=0.0625, scalar2=None, op0=ALU.mult),
          r=["DEC"], w=["DEC"])
    P.add("act", lambda e: e.activation(out=SMALL[:, 0:16], in_=LG[:, 0:16], func=AF.Exp, scale=128.0), r=["LG"], w=["SMALL"])
    P.add("act", lambda e: e.activation(out=SMALL[:, 16:32], in_=LG[:, 0:16], func=AF.Exp, scale=512.0), r=["LG"], w=["SMALL"])
    for h in range(H if 'tab' not in _SKIP else 0):
        P.add("act", lambda e, h=h: e.activation(out=QD[:, h, :], in_=CST[:, 512:640], func=AF.Exp, scale=LG[:, h:h + 1]),
              r=["CST", "LG"], w=["QD"])
        P.add("act", lambda e, h=h: e.activation(out=QD[:, 8 + h, :], in_=CST[:, 640:768], func=AF.Exp, scale=LG[:, 8 + h:9 + h]),
              r=["CST", "LG"], w=["QD"])
        P.add("act", lambda e, h=h: e.activation(out=TMP[1][:, 0:128], in_=CST[:, 0:128], func=AF.Exp, scale=LG[:, h:h + 1]),
              r=["CST", "LG"], w=["TMP1"])
        P.add("act", lambda e, h=h: e.activation(out=TMP[1][:, 128:256], in_=CST[:, 128:256], func=AF.Exp, scale=LG[:, 8 + h:9 + h]),
              r=["CST", "LG"], w=["TMP1"])
        P.add("dve", lambda e: e.tensor_tensor(out=TMP[1][:, 0:256], in0=TMP[1][:, 0:256], in1=CST[:, 256:512], op=ALU.mult),
              r=["TMP1", "CST"], w=["TMP1"])
        P.add("dve", lambda e, h=h: e.tensor_tensor(out=DM[:, h, :], in0=TMP[1][:, 0:128], in1=TMP[1][:, 128:256], op=ALU.add),
              r=["TMP1"], w=["DM"])

    fe_state = {"n": 0}

    def load_gain_table(vec_ap, key):
        dma("sp", GT[:], vec_ap.partition_broadcast(128), r=(), w=["GT"], sem="gt")

    def front_end(src, src_key, row0, nrows, use_mask):
        for (r0, nb, c0) in _blocks(row0, nrows):
            i = fe_state["n"] % 2
            fe_state["n"] += 1
            xin = XIN[i]
            dma("sp", xin[0:nb, :], src[r0:r0 + nb, :], r=[src_key], w=["XIN%d" % i], sem="xin%d" % i)
            ss = SMALL[:, 32 + i:33 + i]
            P.add("act", lambda e, xin=xin, nb=nb, ss=ss: e.activation(out=XN[0:nb, :], in_=xin[0:nb, :], func=AF.Square, accum_out=ss[0:nb, :]),
                  r=["XIN%d" % i], w=["XN", "SS%d" % i])
            P.add("act", lambda e, nb=nb, ss=ss: e.activation(out=ss[0:nb, :], in_=ss[0:nb, :], func=AF.Sqrt, scale=1.0 / D, bias=EPSC[0:nb, 0:1]),
                  r=["SS%d" % i, "EPSC"], w=["SS%d" % i])
            P.add("dve", lambda e, nb=nb, ss=ss: e.reciprocal(out=ss[0:nb, :], in_=ss[0:nb, :]),
                  r=["SS%d" % i], w=["SS%d" % i])
            if use_mask:
                mk = SMALL[:, 34 + i:35 + i]
                dma("sp", mk[0:nb, :], tokmask[r0:r0 + nb, :], r=(), w=["MK%d" % i], sem="mk%d" % i)
                P.add("dve", lambda e, nb=nb, ss=ss, mk=mk: e.tensor_tensor(out=ss[0:nb, :], in0=ss[0:nb, :], in1=mk[0:nb, :], op=ALU.mult),
                      r=["SS%d" % i, "MK%d" % i], w=["SS%d" % i])
            P.add("dve", lambda e, xin=xin, nb=nb, ss=ss: e.scalar_tensor_tensor(out=XN[0:nb, :], in0=xin[0:nb, :], scalar=ss[0:nb, :], in1=GT[0:nb, :], op0=ALU.mult, op1=ALU.mult),
                  r=["XIN%d" % i, "SS%d" % i, "GT", "XN"], w=["XN"])
            for half in range(2):
                pt = PT[half]

                def tr(e, nb=nb, half=half, pt=pt):
                    ins = None
                    for c in range(8):
                        cc = half * 8 + c
                        ins = e.transpose(out=pt[:, c * 128:c * 128 + nb], in_=XN[0:nb, cc * 128:(cc + 1) * 128], identity=IDB[0:nb, 0:nb])
                    return ins
                P.add("pe", tr, r=["XN", "IDB"], w=["PT%d" % half])
                P.add("act" if half == 0 else "dve",
                      (lambda e, nb=nb, half=half, pt=pt, c0=c0: e.activation(
                          out=HN[:, half * 8:half * 8 + 8, c0:c0 + nb],
                          in_=pt[:, :].rearrange("p (c n) -> p c n", c=8)[:, :, 0:nb], func=AF.Copy)) if half == 0 else
                      (lambda e, nb=nb, half=half, pt=pt, c0=c0: e.tensor_copy(
                          out=HN[:, half * 8:half * 8 + 8, c0:c0 + nb],
                          in_=pt[:, :].rearrange("p (c n) -> p c n", c=8)[:, :, 0:nb])),
                      r=["PT%d" % half], w=["HN"])

    be_state = {"n": 0}

    def back_end(actT, act_key, KC, kct, wb, wkey, res_src, res_key, res_row0, dst, dst_key, dst_row0, ntok):
        blks = _blocks(0, ntok)
        kct = 8
        nkt = (KC + 7) // 8
        for nb_ in range(4):
            j = be_state["n"] % 2
            be_state["n"] += 1
            res = RES[j]
            for kt in range(nkt):
                kcn = min(kct, KC - kt * kct)
                slot = load_w([(kcn, 0, 512, wtile(wb, kt * kct, kcn, nb_ * 512, 512))], wkey)
                for (o, nbk, _) in blks:
                    bi = o // 128

                    def mm(e, slot=slot, o=o, nbk=nbk, bi=bi, kt=kt, kcn=kcn):
                        ins = None
                        for kc in range(kcn):
                            kk = kt * kct + kc
                            ins = e.matmul(PS[bi][0:nbk, :], lhsT=actT[:, kk, o:o + nbk], rhs=slot.ap(kc, 0, 512),
                                           start=(kk == 0), stop=(kk == KC - 1))
                        return ins
                    P.add("pe", mm, r=slot.keys + [act_key], w=["PS%d" % bi])
            for (o, nbk, _) in blks:
                bi = o // 128
                dma("sp", res[0:nbk, bi, :], res_src[res_row0 + o:res_row0 + o + nbk, nb_ * 512:(nb_ + 1) * 512],
                    r=[res_key], w=["RES0"], sem="res0")
            for (o, nbk, _) in blks:
                bi = o // 128
                P.add("dve", lambda e, res=res, nbk=nbk, bi=bi: e.tensor_tensor(out=res[0:nbk, bi, :], in0=PS[bi][0:nbk, :], in1=res[0:nbk, bi, :], op=ALU.add),
                      r=["PS%d" % bi, "RES0"], w=["RES0"])
            for (o, nbk, _) in blks:
                bi = o // 128
                dma("pool", dst[dst_row0 + o:dst_row0 + o + nbk, nb_ * 512:(nb_ + 1) * 512], res[0:nbk, bi, :],
                    r=["RES0"], w=[dst_key], sem="resst0")

    QK = [BIG2[:, i * 2048:(i + 1) * 2048].rearrange("p (c n) -> p c n", c=4) for i in range(2)]
    VH = [BIG2[:, 4096 + i * 2048:4096 + (i + 1) * 2048].rearrange("p (c n) -> p c n", c=4) for i in range(2)]
    SGH = [BIG2[:, 8192 + i * 2048:8192 + (i + 1) * 2048].rearrange("p (c n) -> p c n", c=4) for i in range(2)]
    KTF = [BIG2[:, 12288 + i * 1024:12288 + (i + 1) * 1024].rearrange("p (c n) -> p c n", c=4) for i in range(2)]
    KTB = [BIG2[:, 14336 + i * 1024:14336 + (i + 1) * 1024].rearrange("p (c n) -> p c n", c=4) for i in range(2)]
    COS = BIG1[:, 0:1024].bitcast(F32)
    SIN = BIG1[:, 1024:2048].bitcast(F32)
    FBO = [BIG1[:, 2048 + i * 2048:2048 + (i + 1) * 2048].bitcast(F32).rearrange("p (c n) -> p c n", c=2) for i in range(2)]

    load_gain_table(ln_mix[0:1, :], "gt")
    hcount = 0
    for t in range(NTILE if stop_after != "pre" else 0):
        t0 = t * T
        front_end(x_in, "x_in", t0, T, False)
        dma("sp", COS, cos_d[:, t0:t0 + T], r=(), w=["COS"], sem="cos")
        dma("sp", SIN, sin_d[:, t0:t0 + T], r=(), w=["SIN"], sem="sin")
        for h in range(H):
            b = hcount % 2
            hcount += 1
            slot = load_w([(16, 0, 256, wtile(wb_ret_in, 0, 16, h * 256, 256)),
                           (16, 256, 256, wtile(wb_ret_in, 0, 16, HQ + h * 256, 256))], "wb_ret_in")
            for bk in range(4):
                def mm(e, slot=slot, bk=bk):
                    ins = None
                    for kc in range(16):
                        ins = e.matmul(PS[bk][:, :], lhsT=slot.ap(kc, bk * 128, (bk + 1) * 128), rhs=HN[:, kc, 0:T],
                                       start=(kc == 0), stop=(kc == 15))
                    return ins
                P.add("pe", mm, r=slot.keys + ["HN"], w=["PS%d" % bk])
            for qk in range(2):
                p1, p2 = PS[qk * 2], PS[qk * 2 + 1]
                k1, k2 = "PS%d" % (qk * 2), "PS%d" % (qk * 2 + 1)
                P.add("dve", lambda e, p1=p1: e.tensor_tensor(out=TMP[0][:], in0=p1[:, :], in1=COS, op=ALU.mult), r=[k1, "COS"], w=["TMP0"])
                P.add("dve", lambda e, p2=p2: e.tensor_tensor(out=TMP[1][:], in0=p2[:, :], in1=SIN, op=ALU.mult), r=[k2, "SIN"], w=["TMP1"])
                P.add("dve", lambda e, p2=p2: e.tensor_tensor(out=TMP[2][:], in0=p2[:, :], in1=COS, op=ALU.mult), r=[k2, "COS"], w=["TMP2"])
                P.add("dve", lambda e, p1=p1: e.tensor_tensor(out=TMP[3][:], in0=p1[:, :], in1=SIN, op=ALU.mult), r=[k1, "SIN"], w=["TMP3"])
                P.add("dve", lambda e, b=b, qk=qk: e.tensor_tensor(out=QK[b][:, qk * 2, :], in0=TMP[0][:], in1=TMP[1][:], op=ALU.subtract),
                      r=["TMP0", "TMP1"], w=["QK%d" % b])
                P.add("dve", lambda e, b=b, qk=qk: e.tensor_tensor(out=QK[b][:, qk * 2 + 1, :], in0=TMP[2][:], in1=TMP[3][:], op=ALU.add),
                      r=["TMP2", "TMP3"], w=["QK%d" % b])
            dma("pool", qT_s[h, :, :, t0:t0 + T].rearrange("c p n -> p c n"), QK[b][:, 0:2, :], r=["QK%d" % b], w=["qT_s"], sem="qkst%d" % b)
            dma("pool", kT_s[h, :, :, t0:t0 + T].rearrange("c p n -> p c n"), QK[b][:, 2:4, :], r=["QK%d" % b], w=["kT_s"], sem="qkst%d" % b)
            for which in range(2):
                col0 = 2 * HQ + which * HV + h * DV
                slot = load_w([(16, 0, 512, wtile(wb_ret_in, 0, 16, col0, 512))], "wb_ret_in")
                dstb = VH[b] if which == 0 else SGH[b]
                dkey = ("VH%d" if which == 0 else "SGH%d") % b
                for tb in range(4):
                    def mm(e, slot=slot, tb=tb):
                        ins = None
                        for kc in range(16):
                            ins = e.matmul(PS[tb][:, :], lhsT=HN[:, kc, tb * 128:(tb + 1) * 128], rhs=slot.ap(kc, 0, 512),
                                           start=(kc == 0), stop=(kc == 15))
                        return ins
                    P.add("pe", mm, r=slot.keys + ["HN"], w=["PS%d" % tb])
                    P.add("act", lambda e, tb=tb, dstb=dstb, which=which: e.activation(out=dstb[:, tb, :], in_=PS[tb][:, :], func=(AF.Copy if which == 0 else AF.Silu)),
                          r=["PS%d" % tb], w=[dkey])
                dst_d = v_s if which == 0 else sg_s
                dma("pool", dst_d[h, t0:t0 + T, :].rearrange("(tb p) n -> p tb n", p=128), dstb[:, :, :], r=[dkey], w=["v_s" if which == 0 else "sg_s"],
                    sem=("vst%d" if which == 0 else "sgst%d") % b)
            for c in range(4 if 'A4' not in _SKIP else 0):
                def tr(e, b=b, c=c):
                    ins = None
                    for dc in range(2):
                        ins = e.transpose(out=PT[0][:, dc * 128:(dc + 1) * 128], in_=QK[b][:, 2 + dc, c * 128:(c + 1) * 128], identity=IDB[:, :])
                    return ins
                P.add("pe", tr, r=["QK%d" % b, "IDB"], w=["PT0"])
                P.add("act", lambda e, b=b, c=c, h=h: e.activation(out=KTF[b][:, c, :], in_=PT[0][:, 0:256], func=AF.Identity, scale=DEC[:, c * 8 + h:c * 8 + h + 1]),
                      r=["PT0", "DEC"], w=["KTF%d" % b])
                P.add("act", lambda e, b=b, c=c, h=h: e.activation(out=KTB[b][:, c, :], in_=PT[0][:, 0:256], func=AF.Identity, scale=DEC[:, 32 + c * 8 + h:32 + c * 8 + h + 1]),
                      r=["PT0", "DEC"], w=["KTB%d" % b])
            for dr in range(2 if ('A4' not in _SKIP and 'A4b' not in _SKIP) else 0):
                kt_ = KTF[b] if dr == 0 else KTB[b]
                kkey = ("KTF%d" if dr == 0 else "KTB%d") % b
                for dc in range(2):
                    def mm(e, kt_=kt_, dc=dc, b=b):
                        ins = None
                        for c in range(4):
                            ins = e.matmul(PS[4 + dc][:, :], lhsT=kt_[:, c, dc * 128:(dc + 1) * 128], rhs=VH[b][:, c, :], start=(c == 0), stop=(c == 3))
                        return ins
                    P.add("pe", mm, r=[kkey, "VH%d" % b], w=["PS%d" % (4 + dc)])
                    P.add("act" if dc == 0 else "dve",
                          (lambda e, dr=dr, dc=dc: e.activation(out=FBO[dr][:, dc, :], in_=PS[4 + dc][:, :], func=AF.Copy)) if dc == 0 else
                          (lambda e, dr=dr, dc=dc: e.tensor_copy(out=FBO[dr][:, dc, :], in_=PS[4 + dc][:, :])),
                          r=["PS%d" % (4 + dc)], w=["FBO%d" % dr])
                dst_d = F_s if dr == 0 else B_s
                dma("pool", dst_d[t, :, h * 2:h * 2 + 2, :], FBO[dr][:, :, :], r=["FBO%d" % dr], w=["F_s" if dr == 0 else "B_s"], sem="fbst%d" % dr)
            if hcount % 6 == 0:
                emit_casts(1)

    P.barrier()
    SINF = BIG1[:, 0:16384].bitcast(F32).rearrange("p (c n) -> p c n", c=16)
    SINB = BIG2[:, 0:16384].bitcast(F32).rearrange("p (c n) -> p c n", c=16)
    PCB = PC[:, 3072:6144].bitcast(BF16)
    KF_ = PCB[:, 0:1024].rearrange("p (c n) -> p c n", c=2)
    VF_ = PCB[:, 1024:3072].rearrange("p (c n) -> p c n", c=4)
    KTf = PCB[:, 3072:4096].rearrange("p (c n) -> p c n", c=4)
    COSF = RES[0][:, 0, :]
    SINF_ = RES[0][:, 1, :]
    FTAB = PC[:, 6144:6144 + NF * 12]
    KDEC = PC[:, 6500:6532]
    COEF = PC[:, 6532:6548]
    if stop_after != "pre":
        P.add("dve", lambda e: e.memset(SINF[:, :, :], 0.0), w=["SINF"])
        P.add("dve", lambda e: e.memset(SINB[:, :, :], 0.0), w=["SINB"])
        dma("sp", FTAB, ftab_d[:, :], r=(), w=["FTAB"], sem="ftab")
        load_gain_table(ln_mix[0:1, :], "gt")
    for tau in range(NF if stop_after != "pre" else 0):
        t0 = tau * T
        front_end(xf_in, "xf_in", t0, T, False)
        dma("sp", COSF, cosf_d[:, t0:t0 + T], r=(), w=["COSF"], sem="cosf")
        dma("sp", SINF_, sinf_d[:, t0:t0 + T], r=(), w=["SINF_"], sem="sinf")
        fb = tau * 12
        for c in range(4):
            P.add("act", lambda e, c=c, fb=fb: e.activation(out=KDEC[:, c * 8:c * 8 + 8], in_=LG[:, 0:8], func=AF.Identity, scale=FTAB[:, fb + c:fb + c + 1]),
                  r=["LG", "FTAB"], w=["KDEC"])
            P.add("dve", lambda e, c=c, fb=fb: e.scalar_tensor_tensor(out=KDEC[:, c * 8:c * 8 + 8], in0=LG[:, 8:16], scalar=FTAB[:, fb + 4 + c:fb + 5 + c],
                                                                   in1=KDEC[:, c * 8:c * 8 + 8], op0=ALU.mult, op1=ALU.add), r=["LG", "FTAB", "KDEC"], w=["KDEC"])
        P.add("act", lambda e: e.activation(out=KDEC[:, 0:32], in_=KDEC[:, 0:32], func=AF.Exp), r=["KDEC"], w=["KDEC"])
        P.add("dve", lambda e: e.tensor_scalar(out=KDEC[:, 0:32], in0=KDEC[:, 0:32], scalar1=0.0625, scalar2=None, op0=ALU.mult), r=["KDEC"], w=["KDEC"])
        for dr in range(2):
            P.add("act", lambda e, dr=dr, fb=fb: e.activation(out=COEF[:, dr * 8:dr * 8 + 8], in_=LG[:, dr * 8:dr * 8 + 8], func=AF.Exp, scale=FTAB[:, fb + 8 + dr:fb + 9 + dr]),
                  r=["LG", "FTAB"], w=["COEF"])
            P.add("dve", lambda e, dr=dr, fb=fb: e.tensor_tensor(out=COEF[:, dr * 8:dr * 8 + 8], in0=COEF[:, dr * 8:dr * 8 + 8],
                                                               in1=FTAB[:, fb + 10 + dr:fb + 11 + dr].to_broadcast([128, 8]), op=ALU.mult),
                  r=["COEF", "FTAB"], w=["COEF"])
        for h in range(H):
            if (tau * H + h) % 16 == 15:
                emit_casts(1)
            slot = load_w([(16, 0, 256, wtile(wb_ret_in, 0, 16, HQ + h * 256, 256))], "wb_ret_in")
            for bk in range(2):
                def mm(e, slot=slot, bk=bk):
                    ins = None
                    for kc in range(16):
                        ins = e.matmul(PS[bk][:, :], lhsT=slot.ap(kc, bk * 128, (bk + 1) * 128), rhs=HN[:, kc, 0:T], start=(kc == 0), stop=(kc == 15))
                    return ins
                P.add("pe", mm, r=slot.keys + ["HN"], w=["PS%d" % bk])
            P.add("dve", lambda e: e.tensor_tensor(out=TMP[0][:], in0=PS[0][:, :], in1=COSF, op=ALU.mult), r=["PS0", "COSF"], w=["TMP0"])
            P.add("dve", lambda e: e.tensor_tensor(out=TMP[1][:], in0=PS[1][:, :], in1=SINF_, op=ALU.mult), r=["PS1", "SINF_"], w=["TMP1"])
            P.add("dve", lambda e: e.tensor_tensor(out=TMP[2][:], in0=PS[1][:, :], in1=COSF, op=ALU.mult), r=["PS1", "COSF"], w=["TMP2"])
            P.add("dve", lambda e: e.tensor_tensor(out=TMP[3][:], in0=PS[0][:, :], in1=SINF_, op=ALU.mult), r=["PS0", "SINF_"], w=["TMP3"])
            P.add("dve", lambda e: e.tensor_tensor(out=KF_[:, 0, :], in0=TMP[0][:], in1=TMP[1][:], op=ALU.subtract), r=["TMP0", "TMP1"], w=["KF"])
            P.add("dve", lambda e: e.tensor_tensor(out=KF_[:, 1, :], in0=TMP[2][:], in1=TMP[3][:], op=ALU.add), r=["TMP2", "TMP3"], w=["KF"])
            slot = load_w([(16, 0, 512, wtile(wb_ret_in, 0, 16, 2 * HQ + h * DV, 512))], "wb_ret_in")
            for tb in range(4):
                def mm(e, slot=slot, tb=tb):
                    ins = None
                    for kc in range(16):
                        ins = e.matmul(PS[2 + tb % 2][:, :], lhsT=HN[:, kc, tb * 128:(tb + 1) * 128], rhs=slot.ap(kc, 0, 512), start=(kc == 0), stop=(kc == 15))
                    return ins
                P.add("pe", mm, r=slot.keys + ["HN"], w=["PS%d" % (2 + tb % 2)])
                P.add("act", lambda e, tb=tb: e.activation(out=VF_[:, tb, :], in_=PS[2 + tb % 2][:, :], func=AF.Copy), r=["PS%d" % (2 + tb % 2)], w=["VF"])
            for c in range(4):
                def tr(e, c=c):
                    ins = None
                    for dc in range(2):
                        ins = e.transpose(out=PT[0][:, dc * 128:(dc + 1) * 128], in_=KF_[:, dc, c * 128:(c + 1) * 128], identity=IDB[:, :])
                    return ins
                P.add("pe", tr, r=["KF", "IDB"], w=["PT0"])
                P.add("act", lambda e, c=c, h=h: e.activation(out=KTf[:, c, :], in_=PT[0][:, 0:256], func=AF.Identity, scale=KDEC[:, c * 8 + h:c * 8 + h + 1]),
                      r=["PT0", "KDEC"], w=["KTf"])
            for dc in range(2):
                def mm(e, dc=dc):
                    ins = None
                    for c in range(4):
                        ins = e.matmul(PS[4 + dc][:, :], lhsT=KTf[:, c, dc * 128:(dc + 1) * 128], rhs=VF_[:, c, :], start=(c == 0), stop=(c == 3))
                    return ins
                P.add("pe", mm, r=["KTf", "VF"], w=["PS%d" % (4 + dc)])
                P.add("dve", lambda e, dc=dc, h=h: e.scalar_tensor_tensor(out=SINF[:, h * 2 + dc, :], in0=PS[4 + dc][:, :], scalar=COEF[:, h:h + 1], in1=SINF[:, h * 2 + dc, :],
                                                                       op0=ALU.mult, op1=ALU.add), r=["PS%d" % (4 + dc), "COEF", "SINF"], w=["SINF"])
                P.add("dve", lambda e, dc=dc, h=h: e.scalar_tensor_tensor(out=SINB[:, h * 2 + dc, :], in0=PS[4 + dc][:, :], scalar=COEF[:, 8 + h:9 + h], in1=SINB[:, h * 2 + dc, :],
                                                                       op0=ALU.mult, op1=ALU.add), r=["PS%d" % (4 + dc), "COEF", "SINB"], w=["SINB"])
    if stop_after != "pre":
        dma("pool", sinb_s[:, :, :], SINB[:, :, :], r=["SINB"], w=["sinb_s"], sem="sinbst")
    P.barrier()
    SS_ = BIG1[:, 0:16384].bitcast(F32).rearrange("p (c n) -> p c n", c=16)
    SC_ = [BIG1[:, 16384 + i * 2048:16384 + (i + 1) * 2048].rearrange("p (c n) -> p c n", c=4) for i in range(2)]
    FL_ = BIG2[:, 0:16384].bitcast(F32).rearrange("p (c n) -> p c n", c=16)
    for dr in range(2 if stop_after not in ("pre", "A") else 0):
        if dr == 1:
            dma("sp", SS_[:, :, :], sinb_s[:, :, :], r=["sinb_s"], w=["SS0", "SS1", "SS2", "SS3"], sem="sinbl")
        order = list(range(NTILE)) if dr == 0 else list(range(NTILE - 1, -1, -1))
        st_d = Sf_s if dr == 0 else Sb_s
        fb_d = F_s if dr == 0 else B_s
        for t in order:
            for q4 in range(4):
                hh = q4 % 2
                P.add("act", lambda e, hh=hh, q4=q4: e.activation(out=SC_[hh][:, :, :], in_=SS_[:, q4 * 4:q4 * 4 + 4, :], func=AF.Copy),
                      r=["SS%d" % q4], w=["SC%d" % hh])
                dma("pool", st_d[t, :, q4 * 4:q4 * 4 + 4, :], SC_[hh][:, :, :], r=["SC%d" % hh], w=["S_s"], sem="scst%d" % hh)
            for hf in range(2):
                dma("sp", FL_[:, hf * 8:hf * 8 + 8, :], fb_d[t, :, hf * 8:hf * 8 + 8, :], r=["F_s", "B_s"], w=["FL%d" % hf], sem="fl%d" % hf)
            for h in range(H):
                P.add("dve", lambda e, h=h, dr=dr: e.scalar_tensor_tensor(out=SS_[:, h * 2:h * 2 + 2, :], in0=SS_[:, h * 2:h * 2 + 2, :],
                                                                         scalar=SMALL[:, 16 + dr * 8 + h:17 + dr * 8 + h], in1=FL_[:, h * 2:h * 2 + 2, :],
                                                                         op0=ALU.mult, op1=ALU.add),
                      r=["SS%d" % (h // 2), "FL%d" % (h // 4), "SMALL"], w=["SS%d" % (h // 2)])
    P.barrier()
    GATT = BIG1[:, 0:16384].rearrange("p (c n) -> p c n", c=32)
    QB = [BIG2[:, i * 1024:(i + 1) * 1024].rearrange("p (c n) -> p c n", c=2) for i in range(2)]
    KB = [BIG2[:, 2048 + i * 1024:2048 + (i + 1) * 1024].rearrange("p (c n) -> p c n", c=2) for i in range(2)]
    VB = [BIG2[:, 4096 + i * 2048:4096 + (i + 1) * 2048].rearrange("p (c n) -> p c n", c=4) for i in range(2)]
    SGB = [BIG2[:, 8192 + i * 2048:8192 + (i + 1) * 2048].rearrange("p (c n) -> p c n", c=4) for i in range(2)]
    PCB2 = PC[:, 3072:7168].bitcast(BF16)
    SFc = [[BIG2[:, 12288 + c * 1024:12288 + (c + 1) * 1024].rearrange("p (c n) -> p c n", c=2) for c in range(4)],
           [PCB2[:, c * 1024:(c + 1) * 1024].rearrange("p (c n) -> p c n", c=2) for c in range(4)]]
    SBc = [[BIG1[:, 16384 + c * 1024:16384 + (c + 1) * 1024].rearrange("p (c n) -> p c n", c=2) for c in range(4)],
           [PCB2[:, 4096 + c * 1024:4096 + (c + 1) * 1024].rearrange("p (c n) -> p c n", c=2) for c in range(4)]]
    QH = [BIG2[:, 16384:18432].rearrange("p (d c n) -> p d c n", d=2, c=2),
          BIG1[:, 20480:22528].rearrange("p (d c n) -> p d c n", d=2, c=2)]
    KTC = [XN[:, i * 256:(i + 1) * 256] for i in range(2)]
    PB_ = HN[:, 0, 0:256].rearrange("p (i n) -> p i n", i=2)
    GAT = [HN[:, 1 + i, 0:512] for i in range(2)]
    Y1 = TMP[2]

    def prep_loads(t, h, b):
        t0 = t * T
        dma("sp", QB[b][:, :, :], qT_s[h, :, :, t0:t0 + T].rearrange("c p n -> p c n"), r=["qT_s"], w=["QB%d" % b], sem="qb%d" % b)
        dma("sp", KB[b][:, :, :], kT_s[h, :, :, t0:t0 + T].rearrange("c p n -> p c n"), r=["kT_s"], w=["KB%d" % b], sem="kb%d" % b)
        dma("sp", VB[b][:, :, :], v_s[h, t0:t0 + T, :].rearrange("(tb p) n -> p tb n", p=128), r=["v_s"], w=["VB%d" % b], sem="vb%d" % b)
        dma("sp", SGB[b][:, :, :], sg_s[h, t0:t0 + T, :].rearrange("(tb p) n -> p tb n", p=128), r=["sg_s"], w=["SGB%d" % b], sem="sgb%d" % b)
        dma("sp", SFc[b][0][:, :, :], Sf_s[t, :, h * 2:h * 2 + 2, :], r=["S_s"], w=["SF%d_0" % b], sem="sfl%d" % b)
        dma("sp", SBc[b][3][:, :, :], Sb_s[t, :, h * 2:h * 2 + 2, :], r=["S_s"], w=["SB%d_3" % b], sem="sbl%d" % b)
        for dr in range(2):
            P.add("dve", lambda e, b=b, dr=dr, h=h: e.tensor_tensor(
                out=QH[b][:, dr, :, :].rearrange("p c (k i) -> p (c k) i", i=128),
                in0=QB[b][:, :, :].rearrange("p c (k i) -> p (c k) i", i=128),
                in1=QD[:, dr * 8 + h:dr * 8 + h + 1, :].to_broadcast([128, 8, 128]), op=ALU.mult),
                r=["QB%d" % b, "QD"], w=["QH%d" % b])

    def chain_step(t, h, b, step):
        for dr in range(2):
            c = step if dr == 0 else 3 - step
            kt = KTC[(step * 2 + dr) % 2]
            ktk = "KTC%d" % ((step * 2 + dr) % 2)

            def tr(e, b=b, c=c):
                ins = None
                for dc in range(2):
                    ins = e.transpose(out=PT[0][:, dc * 128:(dc + 1) * 128], in_=KB[b][:, dc, c * 128:(c + 1) * 128], identity=IDB[:, :])
                return ins
            P.add("pe", tr, r=["KB%d" % b, "IDB"], w=["PT0"])
            P.add("act", lambda e, kt=kt, dr=dr, h=h: e.activation(out=kt[:, :], in_=PT[0][:, 0:256], func=AF.Identity,
                                                                 scale=DEC[:, 64 + dr * 8 + h:65 + dr * 8 + h]),
                  r=["PT0", "DEC"], w=[ktk])
            src_st = SFc[b][c] if dr == 0 else SBc[b][c]
            dst_st = SFc[b][c + 1] if dr == 0 else SBc[b][c - 1]
            skey = ("SF%d_%d" % (b, c)) if dr == 0 else ("SB%d_%d" % (b, c))
            dkey = ("SF%d_%d" % (b, c + 1)) if dr == 0 else ("SB%d_%d" % (b, c - 1))
            for dc in range(2):
                P.add("pe", lambda e, kt=kt, dc=dc, b=b, c=c: e.matmul(PS[4 + dc][:, :], lhsT=kt[:, dc * 128:(dc + 1) * 128], rhs=VB[b][:, c, :], start=True, stop=True),
                      r=[ktk, "VB%d" % b], w=["PS%d" % (4 + dc)])
                P.add("dve", lambda e, dc=dc, src_st=src_st, dst_st=dst_st, dr=dr, h=h: e.scalar_tensor_tensor(
                    out=dst_st[:, dc, :], in0=src_st[:, dc, :], scalar=SMALL[:, dr * 8 + h:dr * 8 + h + 1], in1=PS[4 + dc][:, :],
                    op0=ALU.mult, op1=ALU.add), r=["PS%d" % (4 + dc), skey, "SMALL"], w=[dkey])

    def out_chunk(t, h, b, c):
        if h % 4 == 0 and c == 0:
            dma("sp", GT[:], gn_gain[0:1, (h // 4) * 2048:(h // 4 + 1) * 2048].partition_broadcast(128), r=(), w=["GT"], sem="gt")
        pi = c % 2

        def sc(e, b=b, c=c):
            ins = None
            for dc in range(2):
                ins = e.matmul(PS[0][:, 0:128], lhsT=KB[b][:, dc, c * 128:(c + 1) * 128], rhs=QB[b][:, dc, c * 128:(c + 1) * 128],
                               start=(dc == 0), stop=(dc == 1))
            return ins
        P.add("pe", sc, r=["KB%d" % b, "QB%d" % b], w=["PS0"])
        P.add("dve", lambda e, pi=pi, h=h: e.tensor_tensor(out=PB_[:, pi, :], in0=PS[0][:, 0:128], in1=DM[:, h, :], op=ALU.mult),
              r=["PS0", "DM"], w=["PB%d" % pi])
        ob = PS[1 + pi]

        def om(e, b=b, c=c, pi=pi, ob=ob):
            e.matmul(ob[:, :], lhsT=PB_[:, pi, :], rhs=VB[b][:, c, :], start=True, stop=False)
            e.matmul(ob[:, :], lhsT=QH[b][:, 0, 0, c * 128:(c + 1) * 128], rhs=SFc[b][c][:, 0, :], start=False, stop=False)
            e.matmul(ob[:, :], lhsT=QH[b][:, 0, 1, c * 128:(c + 1) * 128], rhs=SFc[b][c][:, 1, :], start=False, stop=False)
            e.matmul(ob[:, :], lhsT=QH[b][:, 1, 0, c * 128:(c + 1) * 128], rhs=SBc[b][c][:, 0, :], start=False, stop=False)
            return e.matmul(ob[:, :], lhsT=QH[b][:, 1, 1, c * 128:(c + 1) * 128], rhs=SBc[b][c][:, 1, :], start=False, stop=True)
        P.add("pe", om, r=["PB%d" % pi, "VB%d" % b, "QH%d" % b, "SF%d_%d" % (b, c), "SB%d_%d" % (b, c)], w=["PS%d" % (1 + pi)])
        okey = "PS%d" % (1 + pi)
        P.add("dve", lambda e, ob=ob: e.bn_stats(out=STAT[:, 0:6], in_=ob[:, :]), r=[okey], w=["STAT"])
        P.add("dve", lambda e: e.bn_aggr(out=STAT[:, 8:10], in_=STAT[:, 0:6]), r=["STAT"], w=["STAT2"])
        P.add("act", lambda e: e.activation(out=STAT[:, 10:11], in_=STAT[:, 9:10], func=AF.Sqrt, bias=EPSC[:, 1:2]), r=["STAT2", "EPSC"], w=["STAT3"])
        P.add("dve", lambda e: e.reciprocal(out=STAT[:, 9:10], in_=STAT[:, 10:11]), r=["STAT3", "STAT2"], w=["STAT3"])
        P.add("dve", lambda e, ob=ob, h=h: e.scalar_tensor_tensor(out=Y1[:], in0=ob[:, :], scalar=STAT[:, 8:9], in1=GT[:, (h % 4) * 512:(h % 4 + 1) * 512],
                                                                 op0=ALU.subtract, op1=ALU.mult), r=[okey, "STAT2", "STAT3", "GT"], w=["Y1"])
        P.add("dve", lambda e, b=b, c=c, pi=pi: e.scalar_tensor_tensor(out=GAT[pi], in0=Y1[:], scalar=STAT[:, 9:10], in1=SGB[b][:, c, :],
                                                                    op0=ALU.mult, op1=ALU.mult), r=["Y1", "STAT3", "SGB%d" % b], w=["GAT%d" % pi])

        def trg(e, pi=pi):
            ins = None
            for vc in range(4):
                ins = e.transpose(out=PT[1][:, vc * 128:(vc + 1) * 128], in_=GAT[pi][:, vc * 128:(vc + 1) * 128], identity=IDB[:, :])
            return ins
        P.add("pe", trg, r=["GAT%d" % pi, "IDB"], w=["PT1"])
        P.add("act", lambda e, h=h, c=c: e.activation(out=GATT[:, h * 4:h * 4 + 4, c * 128:(c + 1) * 128],
                                                    in_=PT[1][:, 0:512].rearrange("p (v i) -> p v i", v=4), func=AF.Copy),
              r=["PT1"], w=["GATT"])

    seq_ = [(t, h) for t in range(NTILE if stop_after not in ("pre", "A", "scan") else 0) for h in range(H)]
    if seq_:
        prep_loads(seq_[0][0], seq_[0][1], 0)
        for st_ in range(3):
            chain_step(seq_[0][0], seq_[0][1], 0, st_)
    for i_, (t, h) in enumerate(seq_):
        b = i_ % 2
        nxt = seq_[i_ + 1] if i_ + 1 < len(seq_) else None
        if nxt is not None:
            prep_loads(nxt[0], nxt[1], 1 - b)
        for c in range(4):
            out_chunk(t, h, b, c)
            if nxt is not None and c < 3:
                chain_step(nxt[0], nxt[1], 1 - b, c)
        if i_ % 8 == 7:
            emit_casts(1)
        if h == H - 1:
            back_end(GATT, "GATT", 32, 16, wb_ret_out, "wb_ret_out", x_in, "x_in", t * T, xs[0], "xs0", t * T + 1, T)
    emit_casts(len(cast_pending))
    P.barrier()

    def ffn_phase(li, src, skey, dst, dkey, tok0=0, ntok=None):
        ntok = NT if ntok is None else ntok
        load_gain_table(ln_ffn[li:li + 1, :], "gt")
        cwv = CW[:, :].rearrange("p (c k) -> p c k", k=4)
        for k in range(3):
            dma("sp", cwv[:, :, k], ffn_conv_w[li][k, :].rearrange("(c p) -> p c", p=128), r=(), w=["CW"], sem="cw", slow=True)
        dma("sp", cwv[:, :, 3], ffn_conv_b[li][0, :].rearrange("(c p) -> p c", p=128), r=(), w=["CW"], sem="cw", slow=True)
        HT = BIG1[:, 0:44 * 512].rearrange("p (c n) -> p c n", c=44)
        CT = [BIG2[:, i * 1024:(i + 1) * 1024].bitcast(F32) for i in range(6)]
        for (s0_, sz) in ffn_tiles(ntok):
            s0 = s0_ + tok0
            ncol = sz + 2
            front_end(src, skey, s0, ncol, True)
            for pb in range(22):
                slot = load_w([(16, 0, 256, wtile(wb_ffn_in[li], 0, 16, pb * 256, 256)),
                               (16, 256, 256, wtile(wb_ffn_in[li], 0, 16, FF + pb * 256, 256))], "wb_ffn_in%d" % li)
                for bk in range(4):
                    def mm(e, slot=slot, bk=bk, ncol=ncol):
                        ins = None
                        for kc in range(16):
                            ins = e.matmul(PS[bk][:, 0:ncol], lhsT=slot.ap(kc, bk * 128, (bk + 1) * 128), rhs=HN[:, kc, 0:ncol],
                                           start=(kc == 0), stop=(kc == 15))
                        return ins
                    P.add("pe", mm, r=slot.keys + ["HN"], w=["PS%d" % bk])
                for i in range(2):
                    outs = []
                    for gu in range(2):
                        bk = gu * 2 + i
                        ch = (gu * 44 + pb * 2 + i)
                        ps = PS[bk]
                        pk = "PS%d" % bk
                        ct = CT[gu * 3]
                        ck = "CT%d" % (gu * 3)
                        P.add("act", lambda e, ps=ps, ct=ct, ch=ch, sz=sz: e.activation(out=ct[:, 0:sz], in_=ps[:, 1:sz + 1], func=AF.Identity,
                                                                                     scale=CW[:, ch * 4 + 1:ch * 4 + 2], bias=CW[:, ch * 4 + 3:ch * 4 + 4]),
                              r=[pk, "CW"], w=[ck])
                        ct2 = CT[gu * 3 + 1]
                        ck2 = "CT%d" % (gu * 3 + 1)
                        P.add("dve", lambda e, ps=ps, ct=ct, ct2=ct2, ch=ch, sz=sz: e.scalar_tensor_tensor(out=ct2[:, 0:sz], in0=ps[:, 0:sz], scalar=CW[:, ch * 4:ch * 4 + 1],
                                                                                                      in1=ct[:, 0:sz], op0=ALU.mult, op1=ALU.add),
                              r=[pk, "CW", ck], w=[ck2])
                        ct3 = CT[gu * 3 + 2]
                        ck3 = "CT%d" % (gu * 3 + 2)
                        P.add("dve", lambda e, ps=ps, ct2=ct2, ct3=ct3, ch=ch, sz=sz: e.scalar_tensor_tensor(out=ct3[:, 0:sz], in0=ps[:, 2:sz + 2], scalar=CW[:, ch * 4 + 2:ch * 4 + 3],
                                                                                                        in1=ct2[:, 0:sz], op0=ALU.mult, op1=ALU.add),
                              r=[pk, "CW", ck2], w=[ck3])
                        outs.append((ct3, ck3))
                    (gt_, gk), (ut_, uk) = outs
                    P.add("act", lambda e, gt_=gt_, sz=sz: e.activation(out=CT[0][:, 0:sz], in_=gt_[:, 0:sz], func=AF.Silu), r=[gk], w=["CT0"])
                    P.add("dve", lambda e, ut_=ut_, pb=pb, i=i, sz=sz: e.tensor_tensor(out=HT[:, pb * 2 + i, 0:sz], in0=CT[0][:, 0:sz], in1=ut_[:, 0:sz], op=ALU.mult),
                          r=["CT0", uk], w=["HT"])
            back_end(HT, "HT", 44, 11, wb_ffn_out[li], "wb_ffn_out%d" % li, src, skey, s0 + 1, dst, dkey, s0 + 1, sz)
        P.barrier()

    def sgu_phase(src, skey, dst, dkey):
        TS = 256
        load_gain_table(ln_mix[1:2, :], "gt")
        MX = BIG1[:, 0:48 * TS].rearrange("p (c n) -> p c n", c=48)
        VV = BIG2[:, 0:2 * E].rearrange("p (c n) -> p c n", c=2)
        CC = PC[:, 0:3072].bitcast(BF16).rearrange("p (c n) -> p c n", c=48)
        SEL = PC[0:24, 3072:4608].bitcast(BF16).rearrange("p (c n) -> p c n", c=24)
        LNG = PC[:, 4608:4656]
        LNB = PC[:, 4656:4704]
        BIU = PC[:, 4704:4752]
        BSR = PC[:, 4752:5776].rearrange("p (g i) -> p g i", g=8)
        WST = PC[:, 5776:6288].bitcast(BF16).rearrange("p (g i) -> p g i", g=8)
        BROWF = PC[0:24, 6288:6800]
        BROW = PC[0:24, 6800:7056].bitcast(BF16)
        dma("sp", LNG, sgu_ln_g[0, :].rearrange("(c p) -> p c", p=128), r=(), w=["SGP"], sem="sgp", slow=True)
        dma("sp", LNB, sgu_ln_b[0, :].rearrange("(c p) -> p c", p=128), r=(), w=["SGP"], sem="sgp", slow=True)
        dma("sp", BIU, sgu_b_in[0, 0:E].rearrange("(c p) -> p c", p=128), r=(), w=["SGP"], sem="sgp", slow=True)
        dma("sp", PC[:, 4752:5776], sgu_b_s.rearrange("(o g) i -> o (g i)", o=1).partition_broadcast(128), r=(), w=["SGP"], sem="sgp")
        dma("sp", BROWF, sgu_b_in[0, :].rearrange("(r n) -> r n", n=512), r=(), w=["BROWF"], sem="brow")
        dma("sp", SEL, sel_d[:, :, :], r=(), w=["SEL"], sem="sel")
        P.add("dve", lambda e: e.tensor_copy(out=BROW, in_=BROWF), r=["BROWF"], w=["BROW"])
        P.add("dve", lambda e: e.memset(XN[:, 128:256], 1.0), r=(), w=["XN1"])
        for g in range(G):
            dma("sp", TMP[0][:, 0:128], sgu_w_s[g, :, :], r=(), w=["TMP0"], sem="wsl")
            P.add("dve", lambda e: e.tensor_copy(out=XN[:, 0:128], in_=TMP[0][:, 0:128]), r=["TMP0"], w=["XN"])
            P.add("pe", lambda e: e.transpose(out=PT[0][:, 0:128], in_=XN[:, 0:128], identity=IDB[:, :]), r=["XN", "IDB"], w=["PT0"])
            P.add("act", lambda e, g=g: e.activation(out=WST[:, g, :], in_=PT[0][:, 0:128], func=AF.Copy), r=["PT0"], w=["WST"])
            P.add("pe", lambda e, g=g: e.matmul(PS[0][:, 0:128], lhsT=XN[:, 128:256], rhs=WST[:, g, :], start=True, stop=True), r=["XN1", "WST"], w=["PS0"])
            for k in range(6):
                fc = g * 6 + k
                P.add("dve", lambda e, fc=fc, g=g: e.scalar_tensor_tensor(out=CC[:, fc, :], in0=PS[0][:, 0:128], scalar=LNB[:, fc:fc + 1], in1=BSR[:, g, :],
                                                                       op0=ALU.mult, op1=ALU.add), r=["PS0", "SGP"], w=["CC"])
        for t in range(NT // TS):
            t0 = t * TS
            front_end(src, skey, t0 + 1, TS, False)
            for cb in range(12):
                slot = load_w([(16, 0, 512, wtile(wb_sgu_in, 0, 16, E + cb * 512, 512))], "wb_sgu_in")
                for tb in range(2):
                    bk = (cb % 2) * 2 + tb

                    def mm(e, slot=slot, tb=tb, cb=cb, bk=bk):
                        for kc in range(16):
                            e.matmul(PS[bk][:, :], lhsT=HN[:, kc, tb * 128:(tb + 1) * 128], rhs=slot.ap(kc, 0, 512), start=(kc == 0), stop=False)
                        return e.matmul(PS[bk][:, :], lhsT=SEL[:, 12 + cb, :], rhs=BROW, start=False, stop=True)
                    P.add("pe", mm, r=slot.keys + ["HN", "SEL", "BROW"], w=["PS%d" % bk])
                    P.add("act", lambda e, tb=tb, cb=cb, bk=bk: e.activation(out=VV[:, tb, cb * 512:(cb + 1) * 512], in_=PS[bk][:, :], func=AF.Gelu),
                          r=["PS%d" % bk], w=["VV%d" % tb])
                    P.add("dve", lambda e, tb=tb, cb=cb: e.bn_stats(out=STAT[:, (tb * 12 + cb) * 6:(tb * 12 + cb + 1) * 6], in_=VV[:, tb, cb * 512:(cb + 1) * 512]),
                          r=["VV%d" % tb], w=["STATV%d" % tb])
            for tb in range(2):
                mv = SMALL[:, 40 + tb * 2:42 + tb * 2]
                P.add("dve", lambda e, tb=tb, mv=mv: e.bn_aggr(out=mv, in_=STAT[:, tb * 72:(tb + 1) * 72]), r=["STATV%d" % tb], w=["MV%d" % tb])
                P.add("act", lambda e, mv=mv: e.activation(out=mv[:, 1:2], in_=mv[:, 1:2], func=AF.Sqrt, bias=EPSC[:, 1:2]), r=["MV%d" % tb, "EPSC"], w=["MVb%d" % tb])
                P.add("dve", lambda e, mv=mv: e.reciprocal(out=mv[:, 1:2], in_=mv[:, 1:2]), r=["MVb%d" % tb], w=["MVb%d" % tb])
                P.add("dve", lambda e, tb=tb, mv=mv: e.tensor_scalar(out=VV[:, tb, :], in0=VV[:, tb, :], scalar1=mv[:, 0:1], scalar2=mv[:, 1:2], op0=ALU.subtract, op1=ALU.mult),
                      r=["MV%d" % tb, "MVb%d" % tb, "VV%d" % tb], w=["VV%d" % tb])
            for fc in range(48):
                g = fc // 6
                bk = 4 + (fc % 2)

                def mm(e, fc=fc, g=g, bk=bk):
                    ins = None
                    for c in range(2):
                        ins = e.matmul(PS[bk][:, c * 128:(c + 1) * 128], lhsT=VV[:, c, fc * 128:(fc + 1) * 128], rhs=WST[:, g, :], start=True, stop=True)
                    return ins
                P.add("pe", mm, r=["VV0", "VV1", "WST"], w=["PS%d" % bk])
                P.add("dve", lambda e, fc=fc, bk=bk: e.scalar_tensor_tensor(out=MX[:, fc, :].rearrange("p (c i) -> p c i", c=2),
                                                                         in0=PS[bk][:, 0:256].rearrange("p (c i) -> p c i", c=2), scalar=LNG[:, fc:fc + 1],
                                                                         in1=CC[:, fc:fc + 1, :].to_broadcast([128, 2, 128]), op0=ALU.mult, op1=ALU.add),
                      r=["PS%d" % bk, "CC", "SGP"], w=["MX"])
            for cb in range(12):
                slot = load_w([(16, 0, 512, wtile(wb_sgu_in, 0, 16, cb * 512, 512))], "wb_sgu_in")
                for bk in range(4):
                    fc = cb * 4 + bk

                    def mm(e, slot=slot, bk=bk):
                        ins = None
                        for kc in range(16):
                            ins = e.matmul(PS[bk][:, 0:TS], lhsT=slot.ap(kc, bk * 128, (bk + 1) * 128), rhs=HN[:, kc, 0:TS], start=(kc == 0), stop=(kc == 15))
                        return ins
                    P.add("pe", mm, r=slot.keys + ["HN"], w=["PS%d" % bk])
                    tm = TMP[bk % 2]
                    P.add("act", lambda e, bk=bk, fc=fc, tm=tm: e.activation(out=tm[:, 0:TS], in_=PS[bk][:, 0:TS], func=AF.Gelu, bias=BIU[:, fc:fc + 1]),
                          r=["PS%d" % bk, "SGP"], w=["TMP%d" % (bk % 2)])
                    P.add("dve", lambda e, fc=fc, tm=tm: e.tensor_tensor(out=MX[:, fc, :], in0=tm[:, 0:TS], in1=MX[:, fc, :], op=ALU.mult),
                          r=["TMP%d" % (bk % 2), "MX"], w=["MX"])
            back_end(MX, "MX", 48, 16, wb_sgu_out, "wb_sgu_out", src, skey, t0 + 1, dst, dkey, t0 + 1, TS)
        P.barrier()

    def final_phase(src, skey, row_off):
        row_off = row_off + EXT
        load_gain_table(ln_final[0:1, :], "gt")
        n = 0
        for (r0, nb, _) in _blocks(0, NOWN):
            i = n % 2
            n += 1
            xin = XIN[i]
            dma("sp", xin[0:nb, :], src[row_off + r0:row_off + r0 + nb, :], r=[skey], w=["XIN%d" % i], sem="xin%d" % i)
            ss = SMALL[:, 32 + i:33 + i]
            P.add("act", lambda e, xin=xin, ss=ss: e.activation(out=XN[:, :], in_=xin[:, :], func=AF.Square, accum_out=ss),
                  r=["XIN%d" % i], w=["XN", "SS%d" % i])
            P.add("act", lambda e, ss=ss: e.activation(out=ss, in_=ss, func=AF.Sqrt, scale=1.0 / D, bias=EPSC[:, 0:1]), r=["SS%d" % i, "EPSC"], w=["SS%d" % i])
            P.add("dve", lambda e, ss=ss: e.reciprocal(out=ss, in_=ss), r=["SS%d" % i], w=["SS%d" % i])
            P.add("dve", lambda e, xin=xin, ss=ss: e.scalar_tensor_tensor(out=xin[:, :], in0=xin[:, :], scalar=ss, in1=GT[:, :], op0=ALU.mult, op1=ALU.mult),
                  r=["XIN%d" % i, "SS%d" % i, "GT"], w=["XIN%d" % i])
            dma("pool", y_out[r0:r0 + nb, :], xin[0:nb, :], r=["XIN%d" % i], w=["y_out"], sem="yst%d" % i)

    if stop_after in ("pre", "A", "scan"):
        final_phase(x_in, "x_in", 0)
    elif stop_after == "mix0":
        final_phase(xs[0], "xs0", 1)
    else:
        ffn_phase(0, xs[0], "xs0", xs[1], "xs1")
        if stop_after == "ffn0":
            final_phase(xs[1], "xs1", 1)
        else:
            sgu_phase(xs[1], "xs1", xs[2], "xs2")
            if stop_after == "sgu":
                final_phase(xs[2], "xs2", 1)
            else:
                ffn_phase(1, xs[2], "xs2", xs[3], "xs3", EXT, NOWN)
                final_phase(xs[3], "xs3", 1)
    P.add("sp", None, r=["y_out"], w=())
    P.barrier()

    P.finalize(nc, sem_alloc)
    blk = es.enter_context(nc.Block())

    @blk.tensor
    def _(e):
        P.emit("pe", e)

    @blk.scalar
    def _(e):
        P.emit("act", e)

    @blk.vector
    def _(e):
        P.emit("dve", e)

    @blk.gpsimd
    def _(e):
        P.emit("pool", e)

    @blk.sync
    def _(e):
        P.emit("sp", e)

    es.close()
    return nc


_WNAMES = ["ln_mix", "ln_ffn", "ret_w_in", "ret_gn_gain", "ret_w_out", "sgu_w_in", "sgu_b_in", "sgu_ln_g", "sgu_ln_b",
           "sgu_w_s", "sgu_b_s", "sgu_w_out", "ln_final"]


def shared_inputs(inp):
    f = lambda a: np.ascontiguousarray(np.asarray(a, dtype=np.float32))
    m = {
        "ln_mix": f(inp["ln_mix"]), "ln_ffn": f(inp["ln_ffn"]),
        "ret_w_in": f(inp["ret_w_in"][0]), "lgf": f(inp["ret_log_decay_fwd"]), "lgb": f(inp["ret_log_decay_bwd"]),
        "gn_gain": f(inp["ret_gn_gain"]), "ret_w_out": f(inp["ret_w_out"][0]),
        "sgu_w_in": f(inp["sgu_w_in"][0]), "sgu_b_in": f(inp["sgu_b_in"]), "sgu_ln_g": f(inp["sgu_ln_g"]),
        "sgu_ln_b": f(inp["sgu_ln_b"]), "sgu_w_s": f(inp["sgu_w_s"][0]), "sgu_b_s": f(inp["sgu_b_s"][0]),
        "sgu_w_out": f(inp["sgu_w_out"][0]), "ln_final": f(inp["ln_final"]).reshape(1, D),
    }
    for i in range(2):
        m["ffn_w_in%d" % i] = f(inp["ffn_w_in"][i])
        m["ffn_conv_w%d" % i] = f(inp["ffn_conv_w"][i])
        m["ffn_conv_b%d" % i] = f(inp["ffn_conv_b"][i]).reshape(1, 2 * FF)
        m["ffn_w_out%d" % i] = f(inp["ffn_w_out"][i])
    m.update(make_consts())
    return m


def core_inputs(seq, a, NOWN, NF):
    L = seq.shape[0]
    NT = NOWN + 2 * EXT
    es_, ee_ = a - EXT, a + NOWN + EXT

    def gather(t0, n):
        out = np.zeros((n, D), np.float32)
        lo, hi = max(t0, 0), min(t0 + n, L)
        if hi > lo:
            out[lo - t0:hi - t0] = seq[lo:hi]
        return out
    d = {}
    d["x"] = gather(es_, NT)
    tk = np.arange(es_ - 1, es_ - 1 + NT + 2)
    d["tokmask"] = ((tk >= 0) & (tk < L)).astype(np.float32).reshape(NT + 2, 1)
    d["cos"], d["sin"] = rope_tables(np.clip(np.arange(es_, ee_), 0, None))
    slots = []
    i = 0
    while es_ - T * i > 0:
        slots.append((es_ - T * (i + 1), True, T * i))
        i += 1
    i = 0
    while ee_ + T * i < L:
        slots.append((ee_ + T * i, False, T * i))
        i += 1
    assert len(slots) <= NF, (len(slots), NF)
    xf = np.zeros((NF * T, D), np.float32)
    pos = np.zeros((NF * T,), np.float32)
    ftab = np.zeros((128, NF * 12), np.float32)
    p = np.arange(128, dtype=np.float32)
    for k, (s0, before, dist) in enumerate(slots):
        xf[k * T:(k + 1) * T] = gather(s0, T)
        pos[k * T:(k + 1) * T] = np.clip(np.arange(s0, s0 + T), 0, None)
        for c in range(4):
            if before:
                ftab[:, k * 12 + c] = 511.0 - (c * 128 + p)
            else:
                ftab[:, k * 12 + 4 + c] = c * 128 + p
        if before:
            ftab[:, k * 12 + 8] = dist
            ftab[:, k * 12 + 10] = 1.0
        else:
            ftab[:, k * 12 + 9] = dist
            ftab[:, k * 12 + 11] = 1.0
    d["xf"] = xf
    d["cosf"], d["sinf"] = rope_tables(pos)
    d["ftab"] = ftab
    return d


def run_sequences(inp, seqs, NOWN, NF, stop_after="all"):
    nc = build(NOWN, NF, stop_after)
    sh = shared_inputs(inp)
    plan = []
    for si, sq in enumerate(seqs):
        for a in range(0, sq.shape[0], NOWN):
            plan.append((si, a))
    in_maps = []
    for (si, a) in plan:
        d = dict(sh)
        d.update(core_inputs(seqs[si], a, NOWN, NF))
        in_maps.append(d)
    res = run_bass_kernel_spmd(nc, in_maps, core_ids=list(range(len(plan))))
    outs = [np.zeros((sq.shape[0], D), np.float32) for sq in seqs]
    for c, (si, a) in enumerate(plan):
        outs[si][a:a + NOWN] = res.results[c]["y"]
    return outs


def kernel(**inputs):
    xp = np.asarray(inputs["x_prompt"], dtype=np.float32)
    xsm = np.asarray(inputs["x_sample"], dtype=np.float32)
    seqs = [xp[i] for i in range(xp.shape[0])] + [xsm[i] for i in range(xsm.shape[0])]
    total = sum(sq.shape[0] for sq in seqs)
    NOWN = total // N_CORES
    maxL = max(sq.shape[0] for sq in seqs)
    NF = max(1, -(-(maxL - NOWN - EXT) // T))
    outs = run_sequences(inputs, seqs, NOWN, NF)
    y_prompt = np.stack(outs[:xp.shape[0]], 0).astype(np.float32)
    y_sample = np.stack(outs[xp.shape[0]:], 0).astype(np.float32)
    return (y_prompt, y_sample)
```

```python
import math
import os
_SKIP = os.environ.get('KSKIP', '').split(',')
import numpy as np
import ml_dtypes
import concourse.bass as bass
import concourse.mybir as mybir
from concourse.bass_utils import run_bass_kernel_spmd

F32 = mybir.dt.float32
BF16 = mybir.dt.bfloat16
AF = mybir.ActivationFunctionType
ALU = mybir.AluOpType

D = 2048
H = 8
DK = 256
DV = 512
HQ = H * DK
HV = H * DV
E = 3 * D
G = 8
FF = 5632
EPS = 1e-6
LN_EPS = 1e-5
T = 512
N_CORES = 8

ENGS = ("pe", "act", "dve", "pool", "sp")
EPOCH = 12000


class Op:
    __slots__ = ("idx", "eng", "fn", "deps", "dma", "pos", "waits", "signal", "event")

    def __init__(self, idx, eng, fn, deps, dma, pos):
        self.idx = idx
        self.eng = eng
        self.fn = fn
        self.deps = deps
        self.dma = dma
        self.pos = pos
        self.waits = []
        self.signal = False
        self.event = None


class Prog:
    def __init__(self):
        self.ops = []
        self.eng_ops = {e: [] for e in ENGS}
        self.last_w = {}
        self.readers = {}
        self.dma_since = []

    def add(self, eng, fn, r=(), w=(), dma=None, extra=()):
        idx = len(self.ops)
        deps = set(extra)
        if dma is not None:
            self.dma_since.append(idx)
        for k in r:
            lw = self.last_w.get(k)
            if lw is not None:
                deps.add(lw)
        for k in w:
            lw = self.last_w.get(k)
            if lw is not None:
                deps.add(lw)
            rd = self.readers.get(k)
            if rd:
                deps.update(rd.values() if isinstance(rd, dict) else rd)
        op = Op(idx, eng, fn, sorted(deps), dma, len(self.eng_ops[eng]))
        self.ops.append(op)
        self.eng_ops[eng].append(op)
        for k in r:
            rd = self.readers.setdefault(k, {})
            key = (eng, idx) if dma is not None else eng
            rd[key] = idx
        for k in w:
            self.last_w[k] = idx
            self.readers[k] = {}
        return idx

    def barrier(self):
        pend = list(self.dma_since)
        self.dma_since = []
        last = []
        for e in ENGS:
            for op in reversed(self.eng_ops[e]):
                if op.fn is not None and op.dma is None:
                    last.append(op.idx)
                    break
        for e in ENGS:
            self.add(e, None, extra=last + pend)
        self.last_w = {}
        self.readers = {}

    def finalize(self, nc, sem_alloc):
        ops = self.ops
        waited = {e: {f: -1 for f in ENGS} for e in ENGS}
        waited_dma = {e: {} for e in ENGS}
        dma_count = {}
        for op in ops:
            if op.dma is not None:
                n = dma_count.get(op.dma, 0)
                dma_count[op.dma] = n + 1
                ep, k = divmod(n, 1000)
                op.event = ((op.dma, ep), (k + 1) * 16)
        for op in ops:
            e = op.eng
            best = {}
            for d in op.deps:
                y = ops[d]
                if y.dma is not None:
                    if y.event[0] not in best or ops[best[y.event[0]]].event[1] < y.event[1]:
                        best[y.event[0]] = d
            for d in op.deps:
                y = ops[d]
                if y.dma is not None and best[y.event[0]] != d:
                    continue
                if y.dma is not None:
                    sem, val = y.event
                    if waited_dma[e].get(sem, 0) >= val:
                        continue
                    waited_dma[e][sem] = val
                    op.waits.append(d)
                    continue
                if y.fn is None and y.eng == e:
                    continue
                if y.eng == e:
                    if e == "pe":
                        continue
                    if op.pos - y.pos > 2:
                        continue
                if waited[e][y.eng] >= y.pos:
                    continue
                waited[e][y.eng] = y.pos
                op.waits.append(d)
                y.signal = True
        self.sems = {}
        for e in ENGS:
            cnt = 0
            for op in self.eng_ops[e]:
                if op.dma is None and op.signal:
                    ep, v = divmod(cnt, EPOCH)
                    op.event = (("eng", e, ep), v + 1)
                    cnt += 1
        keys = set()
        for op in ops:
            if op.event is not None and (op.dma is not None or op.signal):
                keys.add(op.event[0])
        for k in sorted(keys, key=str):
            self.sems[k] = sem_alloc(k)

    def emit(self, eng_name, eng):
        ops = self.ops
        for op in self.eng_ops[eng_name]:
            for d in op.waits:
                y = ops[d]
                eng.wait_ge(self.sems[y.event[0]], y.event[1])
            if op.fn is None:
                if op.signal:
                    eng.sem_inc(self.sems[op.event[0]], 1)
                continue
            ins = op.fn(eng)
            if op.dma is not None:
                ins.then_inc(self.sems[op.event[0]], 16)
            elif op.signal:
                ins.then_inc(self.sems[op.event[0]], 1)


def _blocks(start, n, bs=128):
    out = []
    o = 0
    while o < n:
        b = min(bs, n - o)
        out.append((start + o, b, o))
        o += b
    return out


def ffn_tiles(nt):
    n = max(1, math.ceil(nt / 456))
    base = nt // n
    rem = nt - base * n
    tiles = []
    s = 0
    for i in range(n):
        sz = base + (1 if i < rem else 0)
        tiles.append((s, sz))
        s += sz
    return tiles


def make_consts():
    c = {}
    p = np.arange(128, dtype=np.float32)
    i = np.arange(128, dtype=np.float32)
    diff = i[None, :] - p[:, None]
    cst = np.zeros((128, 1024), np.float32)
    cst[:, 0:128] = np.maximum(diff, 0.0)
    cst[:, 128:256] = np.maximum(-diff, 0.0)
    cst[:, 256:384] = (diff >= 0) / 16.0
    cst[:, 384:512] = (diff < 0) / 16.0
    cst[:, 512:640] = (i + 1.0)[None, :]
    cst[:, 640:768] = (128.0 - i)[None, :]
    for tb in range(4):
        cst[:, 768 + tb] = 511.0 - (tb * 128 + p)
        cst[:, 772 + tb] = tb * 128 + p
    cst[:, 776] = 127.0 - p
    cst[:, 777] = p
    c["cst"] = cst
    ident = np.eye(128, dtype=np.float32)
    c["identb"] = ident.astype(ml_dtypes.bfloat16)
    sel = np.zeros((24, 24, 128), np.float32)
    for r in range(24):
        sel[r, r, :] = 1.0
    c["sel"] = sel.astype(ml_dtypes.bfloat16)
    return c


def rope_tables(pos):
    half = DK // 2
    inv = (10000.0 ** (-(np.arange(half, dtype=np.float32) / np.float32(half)))).astype(np.float32)
    ang = (np.asarray(pos, dtype=np.float32)[None, :] * inv[:, None]).astype(np.float32)
    return np.cos(ang).astype(np.float32), np.sin(ang).astype(np.float32)


EXT = 256


def build(NOWN, NF, stop_after="all"):
    NT = NOWN + 2 * EXT
    assert NT % T == 0
    NTILE = NT // T
    nc = bass.Bass("TRN2", target_bir_lowering=False)
    P = Prog()

    def din(name, shape, dt=F32):
        return nc.dram_tensor(name, list(shape), dt, kind="ExternalInput").ap()

    def dscr(name, shape, dt=F32):
        return nc.dram_tensor(name, list(shape), dt, kind="Internal").ap()

    x_in = din("x", [NT, D])
    tokmask = din("tokmask", [NT + 2, 1])
    ln_mix = din("ln_mix", [2, D])
    ln_ffn = din("ln_ffn", [2, D])
    ret_w_in = din("ret_w_in", [D, 2 * HQ + 2 * HV])
    lgf = din("lgf", [1, H])
    lgb = din("lgb", [1, H])
    gn_gain = din("gn_gain", [1, HV])
    ret_w_out = din("ret_w_out", [HV, D])
    sgu_w_in = din("sgu_w_in", [D, 2 * E])
    sgu_b_in = din("sgu_b_in", [1, 2 * E])
    sgu_ln_g = din("sgu_ln_g", [1, E])
    sgu_ln_b = din("sgu_ln_b", [1, E])
    sgu_w_s = din("sgu_w_s", [G, 128, 128])
    sgu_b_s = din("sgu_b_s", [G, 128])
    sgu_w_out = din("sgu_w_out", [E, D])
    ffn_w_in = [din("ffn_w_in%d" % i, [D, 2 * FF]) for i in range(2)]
    ffn_conv_w = [din("ffn_conv_w%d" % i, [3, 2 * FF]) for i in range(2)]
    ffn_conv_b = [din("ffn_conv_b%d" % i, [1, 2 * FF]) for i in range(2)]
    ffn_w_out = [din("ffn_w_out%d" % i, [FF, D]) for i in range(2)]
    ln_final = din("ln_final", [1, D])
    cst_d = din("cst", [128, 1024])
    identb_d = din("identb", [128, 128], BF16)
    sel_d = din("sel", [24, 24, 128], BF16)
    cos_d = din("cos", [128, NT])
    sin_d = din("sin", [128, NT])
    xf_in = din("xf", [NF * T, D])
    cosf_d = din("cosf", [128, NF * T])
    sinf_d = din("sinf", [128, NF * T])
    ftab_d = din("ftab", [128, NF * 12])
    y_out = nc.dram_tensor("y", [NOWN, D], F32, kind="ExternalOutput").ap()
    sinb_s = dscr("sinb_s", [128, 16, 512])

    wb_ret_in = dscr("wb_ret_in", [D, 2 * HQ + 2 * HV], BF16)
    wb_ret_out = dscr("wb_ret_out", [HV, D], BF16)
    wb_sgu_in = dscr("wb_sgu_in", [D, 2 * E], BF16)
    wb_sgu_out = dscr("wb_sgu_out", [E, D], BF16)
    wb_ffn_in = [dscr("wb_ffn_in%d" % i, [D, 2 * FF], BF16) for i in range(2)]
    wb_ffn_out = [dscr("wb_ffn_out%d" % i, [FF, D], BF16) for i in range(2)]
    xs = [dscr("xs%d" % i, [NT + 2, D]) for i in range(4)]
    qT_s = dscr("qT_s", [H, 2, 128, NT], BF16)
    kT_s = dscr("kT_s", [H, 2, 128, NT], BF16)
    v_s = dscr("v_s", [H, NT, DV], BF16)
    sg_s = dscr("sg_s", [H, NT, DV], BF16)
    F_s = dscr("F_s", [NTILE, 128, 16, 512])
    B_s = dscr("B_s", [NTILE, 128, 16, 512])
    Sf_s = dscr("Sf_s", [NTILE, 128, 16, 512], BF16)
    Sb_s = dscr("Sb_s", [NTILE, 128, 16, 512], BF16)

    import contextlib
    es = contextlib.ExitStack()

    def sb(name, shape, dt=F32):
        return es.enter_context(nc.sbuf_tensor(name, list(shape), dt))

    def pst(name, shape, dt=F32):
        return es.enter_context(nc.psum_tensor(name, list(shape), dt))

    W = [sb("w%d" % i, [128, 8, 512], BF16) for i in range(4)]
    HN = sb("hn", [128, 16, 516], BF16)
    XIN = [sb("xin%d" % i, [128, D]) for i in range(2)]
    XN = sb("xn", [128, D], BF16)
    RES = [sb("res0", [128, 4, 512])] * 2
    BIG1 = sb("big1", [128, 44 * 512], BF16)
    BIG2 = sb("big2", [128, 36 * 512], BF16)
    PC = sb("phaseconst", [128, 7168])
    GT = sb("gt", [128, D])
    TMP = [sb("tmp%d" % i, [128, 512]) for i in range(4)]
    CST = BIG1[:, 8192:10240].bitcast(F32)
    IDB = sb("identb_sb", [128, 128], BF16)
    LG = sb("lg", [128, 16])
    SMALL = sb("small", [128, 64])
    EPSC = sb("epsc", [128, 2])
    STAT = sb("stat", [128, 4 * 12 * 6])
    DEC = sb("dec", [128, 80])
    QD = PC[:, 0:2048].rearrange("p (a n) -> p a n", a=16)
    DM = PC[:, 2048:3072].rearrange("p (a n) -> p a n", a=8)
    CW = sb("convw", [128, 88 * 4])
    PS = [pst("ps%d" % i, [128, 512]) for i in range(6)]
    PT = [pst("pt%d" % i, [128, 1024], BF16) for i in range(2)]

    sem_stack = []

    def sem_alloc(key):
        s = es.enter_context(nc.semaphore("s%d" % len(sem_stack)))
        sem_stack.append(s)
        return s

    def dma(eng, out, in_, r, w, sem, slow=False):
        if 'st' in _SKIP and eng == "pool" and sem not in ("cast", "zp") and not sem.startswith("yst"):
            return
        if slow:
            P.add(eng, lambda e, out=out, in_=in_: e.dma_start(out=out, in_=in_, allow_slow_non_contiguous=True), r=r, w=w, dma=sem)
        else:
            P.add(eng, lambda e, out=out, in_=in_: e.dma_start(out=out, in_=in_), r=r, w=w, dma=sem)

    out_stores = []

    cast_pending = []

    def cast_w(dst, src, key, now=False):
        if 'cast' in _SKIP:
            return
        R, C = src.shape
        for r0 in range(0, R, 2048):
            r1 = min(R, r0 + 2048)
            for c0 in range(0, C, 2048):
                c1 = min(C, c0 + 2048)
                cast_pending.append((dst[r0:r1, c0:c1], src[r0:r1, c0:c1], key))
        if now:
            emit_casts(len(cast_pending))

    def emit_casts(n):
        for _ in range(min(n, len(cast_pending))):
            d_, s_, key = cast_pending.pop(0)
            dma("pool", d_, s_, r=["castchain"], w=["castchain", key], sem="cast")

    wstate = {"n": 0}

    class WT:
        def __init__(self, slots):
            self.slots = slots
            self.keys = ["W%d" % sl for sl in slots]

        def ap(self, kc, c0, c1):
            return W[self.slots[kc // 8]][:, kc % 8, c0:c1]

    def load_w(pieces, key):
        kcmax = max(p[0] for p in pieces)
        nsl = (kcmax + 7) // 8
        slots = []
        for g in range(nsl):
            sl = wstate["n"] % 4
            wstate["n"] += 1
            slots.append(sl)
            for (kc, c0, ncol, ap) in pieces:
                k0, k1 = g * 8, min(kc, g * 8 + 8)
                if k1 > k0:
                    dma("sp", W[sl][:, 0:k1 - k0, c0:c0 + ncol], ap[:, k0:k1, :], r=[key], w=["W%d" % sl], sem="w%d" % sl)
        return WT(slots)

    def wtile(wb, k0, kc, c0, ncol):
        return wb[k0 * 128:(k0 + kc) * 128, c0:c0 + ncol].rearrange("(kc p) n -> p kc n", p=128)

    dma("sp", CST[:], cst_d[:, :], r=(), w=["CST"], sem="cst")
    dma("sp", IDB[:], identb_d[:, :], r=(), w=["IDB"], sem="idb")
    dma("sp", LG[:, 0:8], lgf[0:1, :].partition_broadcast(128), r=(), w=["LG"], sem="lg")
    dma("sp", LG[:, 8:16], lgb[0:1, :].partition_broadcast(128), r=(), w=["LG"], sem="lg")
    P.add("dve", lambda e: e.memset(EPSC[:, 0:1], EPS), w=["EPSC"])
    P.add("dve", lambda e: e.memset(EPSC[:, 1:2], LN_EPS), r=["EPSC"], w=["EPSC"])
    P.add("dve", lambda e: e.memset(TMP[0][:], 0.0), w=["TMP0"])
    for i in range(4 if 'zp' not in _SKIP else 0):
        dma("pool", xs[i][0:1, :].rearrange("a (b c) -> (a b) c", b=4), TMP[0][0:4, :], r=["TMP0"], w=["xs%d" % i], sem="zp")
        dma("pool", xs[i][NT + 1:NT + 2, :].rearrange("a (b c) -> (a b) c", b=4), TMP[0][0:4, :], r=["TMP0"], w=["xs%d" % i], sem="zp")

    cast_w(wb_ret_in, ret_w_in, "wb_ret_in", now=True)
    cast_w(wb_ret_out, ret_w_out, "wb_ret_out")
    cast_w(wb_ffn_in[0], ffn_w_in[0], "wb_ffn_in0")
    cast_w(wb_ffn_out[0], ffn_w_out[0], "wb_ffn_out0")
    cast_w(wb_sgu_in, sgu_w_in, "wb_sgu_in")
    cast_w(wb_sgu_out, sgu_w_out, "wb_sgu_out")
    cast_w(wb_ffn_in[1], ffn_w_in[1], "wb_ffn_in1")
    cast_w(wb_ffn_out[1], ffn_w_out[1], "wb_ffn_out1")

    def exp_tab(out_ap, in_ap, scale_ap, post=None):
        P.add("act", lambda e: e.activation(out=out_ap, in_=in_ap, func=AF.Exp, scale=scale_ap),
              r=["CST", "LG"], w=["DEC"])
    for tb in range(4 if 'tab' not in _SKIP else 0):
        exp_tab(DEC[:, tb * 8:tb * 8 + 8], LG[:, 0:8], CST[:, 768 + tb:769 + tb])
        exp_tab(DEC[:, 32 + tb * 8:32 + tb * 8 + 8], LG[:, 8:16], CST[:, 772 + tb:773 + tb])
    exp_tab(DEC[:, 64:72], LG[:, 0:8], CST[:, 776:777])
    exp_tab(DEC[:, 72:80], LG[:, 8:16], CST[:, 777:778])
    P.add("dve", lambda e: e.tensor_scalar(out=DEC[:, 0:80], in0=DEC[:, 0:80], scalar1=0.0625, scalar2=None, op0=ALU.mult),
          r=["DEC"], w=["DEC"])
    P.add("act", lambda e: e.activation(out=SMALL[:, 0:16], in_=LG[:, 0:16], func=AF.Exp, scale=128.0), r=["LG"], w=["SMALL"])
    P.add("act", lambda e: e.activation(out=SMALL[:, 16:32], in_=LG[:, 0:16], func=AF.Exp, scale=512.0), r=["LG"], w=["SMALL"])
    for h in range(H if 'tab' not in _SKIP else 0):
        P.add("act", lambda e, h=h: e.activation(out=QD[:, h, :], in_=CST[:, 512:640], func=AF.Exp, scale=LG[:, h:h + 1]),
              r=["CST", "LG"], w=["QD"])
        P.add("act", lambda e, h=h: e.activation(out=QD[:, 8 + h, :], in_=CST[:, 640:768], func=AF.Exp, scale=LG[:, 8 + h:9 + h]),
              r=["CST", "LG"], w=["QD"])
        P.add("act", lambda e, h=h: e.activation(out=TMP[1][:, 0:128], in_=CST[:, 0:128], func=AF.Exp, scale=LG[:, h:h + 1]),
              r=["CST", "LG"], w=["TMP1"])
        P.add("act", lambda e, h=h: e.activation(out=TMP[1][:, 128:256], in_=CST[:, 128:256], func=AF.Exp, scale=LG[:, 8 + h:9 + h]),
              r=["CST", "LG"], w=["TMP1"])
        P.add("dve", lambda e: e.tensor_tensor(out=TMP[1][:, 0:256], in0=TMP[1][:, 0:256], in1=CST[:, 256:512], op=ALU.mult),
              r=["TMP1", "CST"], w=["TMP1"])
        P.add("dve", lambda e, h=h: e.tensor_tensor(out=DM[:, h, :], in0=TMP[1][:, 0:128], in1=TMP[1][:, 128:256], op=ALU.add),
              r=["TMP1"], w=["DM"])

    fe_state = {"n": 0}

    def load_gain_table(vec_ap, key):
        dma("sp", GT[:], vec_ap.partition_broadcast(128), r=(), w=["GT"], sem="gt")

    def front_end(src, src_key, row0, nrows, use_mask):
        for (r0, nb, c0) in _blocks(row0, nrows):
            i = fe_state["n"] % 2
            fe_state["n"] += 1
            xin = XIN[i]
            dma("sp", xin[0:nb, :], src[r0:r0 + nb, :], r=[src_key], w=["XIN%d" % i], sem="xin%d" % i)
            ss = SMALL[:, 32 + i:33 + i]
            P.add("act", lambda e, xin=xin, nb=nb, ss=ss: e.activation(out=XN[0:nb, :], in_=xin[0:nb, :], func=AF.Square, accum_out=ss[0:nb, :]),
                  r=["XIN%d" % i], w=["XN", "SS%d" % i])
            P.add("act", lambda e, nb=nb, ss=ss: e.activation(out=ss[0:nb, :], in_=ss[0:nb, :], func=AF.Sqrt, scale=1.0 / D, bias=EPSC[0:nb, 0:1]),
                  r=["SS%d" % i, "EPSC"], w=["SS%d" % i])
            P.add("dve", lambda e, nb=nb, ss=ss: e.reciprocal(out=ss[0:nb, :], in_=ss[0:nb, :]),
                  r=["SS%d" % i], w=["SS%d" % i])
            if use_mask:
                mk = SMALL[:, 34 + i:35 + i]
                dma("sp", mk[0:nb, :], tokmask[r0:r0 + nb, :], r=(), w=["MK%d" % i], sem="mk%d" % i)
                P.add("dve", lambda e, nb=nb, ss=ss, mk=mk: e.tensor_tensor(out=ss[0:nb, :], in0=ss[0:nb, :], in1=mk[0:nb, :], op=ALU.mult),
                      r=["SS%d" % i, "MK%d" % i], w=["SS%d" % i])
            P.add("dve", lambda e, xin=xin, nb=nb, ss=ss: e.scalar_tensor_tensor(out=XN[0:nb, :], in0=xin[0:nb, :], scalar=ss[0:nb, :], in1=GT[0:nb, :], op0=ALU.mult, op1=ALU.mult),
                  r=["XIN%d" % i, "SS%d" % i, "GT", "XN"], w=["XN"])
            for half in range(2):
                pt = PT[half]

                def tr(e, nb=nb, half=half, pt=pt):
                    ins = None
                    for c in range(8):
                        cc = half * 8 + c
                        ins = e.transpose(out=pt[:, c * 128:c * 128 + nb], in_=XN[0:nb, cc * 128:(cc + 1) * 128], identity=IDB[0:nb, 0:nb])
                    return ins
                P.add("pe", tr, r=["XN", "IDB"], w=["PT%d" % half])
                P.add("act" if half == 0 else "dve",
                      (lambda e, nb=nb, half=half, pt=pt, c0=c0: e.activation(
                          out=HN[:, half * 8:half * 8 + 8, c0:c0 + nb],
                          in_=pt[:, :].rearrange("p (c n) -> p c n", c=8)[:, :, 0:nb], func=AF.Copy)) if half == 0 else
                      (lambda e, nb=nb, half=half, pt=pt, c0=c0: e.tensor_copy(
                          out=HN[:, half * 8:half * 8 + 8, c0:c0 + nb],
                          in_=pt[:, :].rearrange("p (c n) -> p c n", c=8)[:, :, 0:nb])),
                      r=["PT%d" % half], w=["HN"])

    be_state = {"n": 0}

    def back_end(actT, act_key, KC, kct, wb, wkey, res_src, res_key, res_row0, dst, dst_key, dst_row0, ntok):
        blks = _blocks(0, ntok)
        kct = 8
        nkt = (KC + 7) // 8
        for nb_ in range(4):
            j = be_state["n"] % 2
            be_state["n"] += 1
            res = RES[j]
            for kt in range(nkt):
                kcn = min(kct, KC - kt * kct)
                slot = load_w([(kcn, 0, 512, wtile(wb, kt * kct, kcn, nb_ * 512, 512))], wkey)
                for (o, nbk, _) in blks:
                    bi = o // 128

                    def mm(e, slot=slot, o=o, nbk=nbk, bi=bi, kt=kt, kcn=kcn):
                        ins = None
                        for kc in range(kcn):
                            kk = kt * kct + kc
                            ins = e.matmul(PS[bi][0:nbk, :], lhsT=actT[:, kk, o:o + nbk], rhs=slot.ap(kc, 0, 512),
                                           start=(kk == 0), stop=(kk == KC - 1))
                        return ins
                    P.add("pe", mm, r=slot.keys + [act_key], w=["PS%d" % bi])
            for (o, nbk, _) in blks:
                bi = o // 128
                dma("sp", res[0:nbk, bi, :], res_src[res_row0 + o:res_row0 + o + nbk, nb_ * 512:(nb_ + 1) * 512],
                    r=[res_key], w=["RES0"], sem="res0")
            for (o, nbk, _) in blks:
                bi = o // 128
                P.add("dve", lambda e, res=res, nbk=nbk, bi=bi: e.tensor_tensor(out=res[0:nbk, bi, :], in0=PS[bi][0:nbk, :], in1=res[0:nbk, bi, :], op=ALU.add),
                      r=["PS%d" % bi, "RES0"], w=["RES0"])
            for (o, nbk, _) in blks:
                bi = o // 128
                dma("pool", dst[dst_row0 + o:dst_row0 + o + nbk, nb_ * 512:(nb_ + 1) * 512], res[0:nbk, bi, :],
                    r=["RES0"], w=[dst_key], sem="resst0")

    QK = [BIG2[:, i * 2048:(i + 1) * 2048].rearrange("p (c n) -> p c n", c=4) for i in range(2)]
    VH = [BIG2[:, 4096 + i * 2048:4096 + (i + 1) * 2048].rearrange("p (c n) -> p c n", c=4) for i in range(2)]
    SGH = [BIG2[:, 8192 + i * 2048:8192 + (i + 1) * 2048].rearrange("p (c n) -> p c n", c=4) for i in range(2)]
    KTF = [BIG2[:, 12288 + i * 1024:12288 + (i + 1) * 1024].rearrange("p (c n) -> p c n", c=4) for i in range(2)]
    KTB = [BIG2[:, 14336 + i * 1024:14336 + (i + 1) * 1024].rearrange("p (c n) -> p c n", c=4) for i in range(2)]
    COS = BIG1[:, 0:1024].bitcast(F32)
    SIN = BIG1[:, 1024:2048].bitcast(F32)
    FBO = [BIG1[:, 2048 + i * 2048:2048 + (i + 1) * 2048].bitcast(F32).rearrange("p (c n) -> p c n", c=2) for i in range(2)]

    load_gain_table(ln_mix[0:1, :], "gt")
    hcount = 0
    for t in range(NTILE if stop_after != "pre" else 0):
        t0 = t * T
        front_end(x_in, "x_in", t0, T, False)
        dma("sp", COS, cos_d[:, t0:t0 + T], r=(), w=["COS"], sem="cos")
        dma("sp", SIN, sin_d[:, t0:t0 + T], r=(), w=["SIN"], sem="sin")
        for h in range(H):
            b = hcount % 2
            hcount += 1
            emit_casts(1)
            slot = load_w([(16, 0, 256, wtile(wb_ret_in, 0, 16, h * 256, 256)),
                           (16, 256, 256, wtile(wb_ret_in, 0, 16, HQ + h * 256, 256))], "wb_ret_in")
            for bk in range(4):
                def mm(e, slot=slot, bk=bk):
                    ins = None
                    for kc in range(16):
                        ins = e.matmul(PS[bk][:, :], lhsT=slot.ap(kc, bk * 128, (bk + 1) * 128), rhs=HN[:, kc, 0:T],
                                       start=(kc == 0), stop=(kc == 15))
                    return ins
                P.add("pe", mm, r=slot.keys + ["HN"], w=["PS%d" % bk])
            for qk in range(2):
                p1, p2 = PS[qk * 2], PS[qk * 2 + 1]
                k1, k2 = "PS%d" % (qk * 2), "PS%d" % (qk * 2 + 1)
                P.add("dve", lambda e, p1=p1: e.tensor_tensor(out=TMP[0][:], in0=p1[:, :], in1=COS, op=ALU.mult), r=[k1, "COS"], w=["TMP0"])
                P.add("dve", lambda e, p2=p2: e.tensor_tensor(out=TMP[1][:], in0=p2[:, :], in1=SIN, op=ALU.mult), r=[k2, "SIN"], w=["TMP1"])
                P.add("dve", lambda e, p2=p2: e.tensor_tensor(out=TMP[2][:], in0=p2[:, :], in1=COS, op=ALU.mult), r=[k2, "COS"], w=["TMP2"])
                P.add("dve", lambda e, p1=p1: e.tensor_tensor(out=TMP[3][:], in0=p1[:, :], in1=SIN, op=ALU.mult), r=[k1, "SIN"], w=["TMP3"])
                P.add("dve", lambda e, b=b, qk=qk: e.tensor_tensor(out=QK[b][:, qk * 2, :], in0=TMP[0][:], in1=TMP[1][:], op=ALU.subtract),
                      r=["TMP0", "TMP1"], w=["QK%d" % b])
                P.add("dve", lambda e, b=b, qk=qk: e.tensor_tensor(out=QK[b][:, qk * 2 + 1, :], in0=TMP[2][:], in1=TMP[3][:], op=ALU.add),
                      r=["TMP2", "TMP3"], w=["QK%d" % b])
            dma("pool", qT_s[h, :, :, t0:t0 + T].rearrange("c p n -> p c n"), QK[b][:, 0:2, :], r=["QK%d" % b], w=["qT_s"], sem="qkst%d" % b)
            dma("pool", kT_s[h, :, :, t0:t0 + T].rearrange("c p n -> p c n"), QK[b][:, 2:4, :], r=["QK%d" % b], w=["kT_s"], sem="qkst%d" % b)
            for which in range(2):
                col0 = 2 * HQ + which * HV + h * DV
                slot = load_w([(16, 0, 512, wtile(wb_ret_in, 0, 16, col0, 512))], "wb_ret_in")
                dstb = VH[b] if which == 0 else SGH[b]
                dkey = ("VH%d" if which == 0 else "SGH%d") % b
                for tb in range(4):
                    def mm(e, slot=slot, tb=tb):
                        ins = None
                        for kc in range(16):
                            ins = e.matmul(PS[tb][:, :], lhsT=HN[:, kc, tb * 128:(tb + 1) * 128], rhs=slot.ap(kc, 0, 512),
                                           start=(kc == 0), stop=(kc == 15))
                        return ins
                    P.add("pe", mm, r=slot.keys + ["HN"], w=["PS%d" % tb])
                    P.add("act", lambda e, tb=tb, dstb=dstb, which=which: e.activation(out=dstb[:, tb, :], in_=PS[tb][:, :], func=(AF.Copy if which == 0 else AF.Silu)),
                          r=["PS%d" % tb], w=[dkey])
                dst_d = v_s if which == 0 else sg_s
                dma("pool", dst_d[h, t0:t0 + T, :].rearrange("(tb p) n -> p tb n", p=128), dstb[:, :, :], r=[dkey], w=["v_s" if which == 0 else "sg_s"],
                    sem=("vst%d" if which == 0 else "sgst%d") % b)
            for c in range(4 if 'A4' not in _SKIP else 0):
                def tr(e, b=b, c=c):
                    ins = None
                    for dc in range(2):
                        ins = e.transpose(out=PT[0][:, dc * 128:(dc + 1) * 128], in_=QK[b][:, 2 + dc, c * 128:(c + 1) * 128], identity=IDB[:, :])
                    return ins
                P.add("pe", tr, r=["QK%d" % b, "IDB"], w=["PT0"])
                P.add("act", lambda e, b=b, c=c, h=h: e.activation(out=KTF[b][:, c, :], in_=PT[0][:, 0:256], func=AF.Identity, scale=DEC[:, c * 8 + h:c * 8 + h + 1]),
                      r=["PT0", "DEC"], w=["KTF%d" % b])
                P.add("act", lambda e, b=b, c=c, h=h: e.activation(out=KTB[b][:, c, :], in_=PT[0][:, 0:256], func=AF.Identity, scale=DEC[:, 32 + c * 8 + h:32 + c * 8 + h + 1]),
                      r=["PT0", "DEC"], w=["KTB%d" % b])
            for dr in range(2 if ('A4' not in _SKIP and 'A4b' not in _SKIP) else 0):
                kt_ = KTF[b] if dr == 0 else KTB[b]
                kkey = ("KTF%d" if dr == 0 else "KTB%d") % b
                for dc in range(2):
                    def mm(e, kt_=kt_, dc=dc, b=b):
                        ins = None
                        for c in range(4):
                            ins = e.matmul(PS[4 + dc][:, :], lhsT=kt_[:, c, dc * 128:(dc + 1) * 128], rhs=VH[b][:, c, :], start=(c == 0), stop=(c == 3))
                        return ins
                    P.add("pe", mm, r=[kkey, "VH%d" % b], w=["PS%d" % (4 + dc)])
                    P.add("act" if dc == 0 else "dve",
                          (lambda e, dr=dr, dc=dc: e.activation(out=FBO[dr][:, dc, :], in_=PS[4 + dc][:, :], func=AF.Copy)) if dc == 0 else
                          (lambda e, dr=dr, dc=dc: e.tensor_copy(out=FBO[dr][:, dc, :], in_=PS[4 + dc][:, :])),
                          r=["PS%d" % (4 + dc)], w=["FBO%d" % dr])
                dst_d = F_s if dr == 0 else B_s
                dma("pool", dst_d[t, :, h * 2:h * 2 + 2, :], FBO[dr][:, :, :], r=["FBO%d" % dr], w=["F_s" if dr == 0 else "B_s"], sem="fbst%d" % dr)

    emit_casts(len(cast_pending))
    P.barrier()
    SINF = BIG1[:, 0:16384].bitcast(F32).rearrange("p (c n) -> p c n", c=16)
    SINB = BIG2[:, 0:16384].bitcast(F32).rearrange("p (c n) -> p c n", c=16)
    PCB = PC[:, 3072:6144].bitcast(BF16)
    KF_ = PCB[:, 0:1024].rearrange("p (c n) -> p c n", c=2)
    VF_ = PCB[:, 1024:3072].rearrange("p (c n) -> p c n", c=4)
    KTf = PCB[:, 3072:4096].rearrange("p (c n) -> p c n", c=4)
    COSF = RES[0][:, 0, :]
    SINF_ = RES[0][:, 1, :]
    FTAB = PC[:, 6144:6144 + NF * 12]
    KDEC = PC[:, 6500:6532]
    COEF = PC[:, 6532:6548]
    if stop_after != "pre":
        P.add("dve", lambda e: e.memset(SINF[:, :, :], 0.0), w=["SINF"])
        P.add("dve", lambda e: e.memset(SINB[:, :, :], 0.0), w=["SINB"])
        dma("sp", FTAB, ftab_d[:, :], r=(), w=["FTAB"], sem="ftab")
        load_gain_table(ln_mix[0:1, :], "gt")
    for tau in range(NF if stop_after != "pre" else 0):
        t0 = tau * T
        front_end(xf_in, "xf_in", t0, T, False)
        dma("sp", COSF, cosf_d[:, t0:t0 + T], r=(), w=["COSF"], sem="cosf")
        dma("sp", SINF_, sinf_d[:, t0:t0 + T], r=(), w=["SINF_"], sem="sinf")
        fb = tau * 12
        for c in range(4):
            P.add("act", lambda e, c=c, fb=fb: e.activation(out=KDEC[:, c * 8:c * 8 + 8], in_=LG[:, 0:8], func=AF.Identity, scale=FTAB[:, fb + c:fb + c + 1]),
                  r=["LG", "FTAB"], w=["KDEC"])
            P.add("dve", lambda e, c=c, fb=fb: e.scalar_tensor_tensor(out=KDEC[:, c * 8:c * 8 + 8], in0=LG[:, 8:16], scalar=FTAB[:, fb + 4 + c:fb + 5 + c],
                                                                   in1=KDEC[:, c * 8:c * 8 + 8], op0=ALU.mult, op1=ALU.add), r=["LG", "FTAB", "KDEC"], w=["KDEC"])
        P.add("act", lambda e: e.activation(out=KDEC[:, 0:32], in_=KDEC[:, 0:32], func=AF.Exp), r=["KDEC"], w=["KDEC"])
        P.add("dve", lambda e: e.tensor_scalar(out=KDEC[:, 0:32], in0=KDEC[:, 0:32], scalar1=0.0625, scalar2=None, op0=ALU.mult), r=["KDEC"], w=["KDEC"])
        for dr in range(2):
            P.add("act", lambda e, dr=dr, fb=fb: e.activation(out=COEF[:, dr * 8:dr * 8 + 8], in_=LG[:, dr * 8:dr * 8 + 8], func=AF.Exp, scale=FTAB[:, fb + 8 + dr:fb + 9 + dr]),
                  r=["LG", "FTAB"], w=["COEF"])
            P.add("dve", lambda e, dr=dr, fb=fb: e.tensor_tensor(out=COEF[:, dr * 8:dr * 8 + 8], in0=COEF[:, dr * 8:dr * 8 + 8],
                                                               in1=FTAB[:, fb + 10 + dr:fb + 11 + dr].to_broadcast([128, 8]), op=ALU.mult),
                  r=["COEF", "FTAB"], w=["COEF"])
        for h in range(H):
            slot = load_w([(16, 0, 256, wtile(wb_ret_in, 0, 16, HQ + h * 256, 256))], "wb_ret_in")
            for bk in range(2):
                def mm(e, slot=slot, bk=bk):
                    ins = None
                    for kc in range(16):
                        ins = e.matmul(PS[bk][:, :], lhsT=slot.ap(kc, bk * 128, (bk + 1) * 128), rhs=HN[:, kc, 0:T], start=(kc == 0), stop=(kc == 15))
                    return ins
                P.add("pe", mm, r=slot.keys + ["HN"], w=["PS%d" % bk])
            P.add("dve", lambda e: e.tensor_tensor(out=TMP[0][:], in0=PS[0][:, :], in1=COSF, op=ALU.mult), r=["PS0", "COSF"], w=["TMP0"])
            P.add("dve", lambda e: e.tensor_tensor(out=TMP[1][:], in0=PS[1][:, :], in1=SINF_, op=ALU.mult), r=["PS1", "SINF_"], w=["TMP1"])
            P.add("dve", lambda e: e.tensor_tensor(out=TMP[2][:], in0=PS[1][:, :], in1=COSF, op=ALU.mult), r=["PS1", "COSF"], w=["TMP2"])
            P.add("dve", lambda e: e.tensor_tensor(out=TMP[3][:], in0=PS[0][:, :], in1=SINF_, op=ALU.mult), r=["PS0", "SINF_"], w=["TMP3"])
            P.add("dve", lambda e: e.tensor_tensor(out=KF_[:, 0, :], in0=TMP[0][:], in1=TMP[1][:], op=ALU.subtract), r=["TMP0", "TMP1"], w=["KF"])
            P.add("dve", lambda e: e.tensor_tensor(out=KF_[:, 1, :], in0=TMP[2][:], in1=TMP[3][:], op=ALU.add), r=["TMP2", "TMP3"], w=["KF"])
            slot = load_w([(16, 0, 512, wtile(wb_ret_in, 0, 16, 2 * HQ + h * DV, 512))], "wb_ret_in")
            for tb in range(4):
                def mm(e, slot=slot, tb=tb):
                    ins = None
                    for kc in range(16):
                        ins = e.matmul(PS[2 + tb % 2][:, :], lhsT=HN[:, kc, tb * 128:(tb + 1) * 128], rhs=slot.ap(kc, 0, 512), start=(kc == 0), stop=(kc == 15))
                    return ins
                P.add("pe", mm, r=slot.keys + ["HN"], w=["PS%d" % (2 + tb % 2)])
                P.add("act", lambda e, tb=tb: e.activation(out=VF_[:, tb, :], in_=PS[2 + tb % 2][:, :], func=AF.Copy), r=["PS%d" % (2 + tb % 2)], w=["VF"])
            for c in range(4):
                def tr(e, c=c):
                    ins = None
                    for dc in range(2):
                        ins = e.transpose(out=PT[0][:, dc * 128:(dc + 1) * 128], in_=KF_[:, dc, c * 128:(c + 1) * 128], identity=IDB[:, :])
                    return ins
                P.add("pe", tr, r=["KF", "IDB"], w=["PT0"])
                P.add("act", lambda e, c=c, h=h: e.activation(out=KTf[:, c, :], in_=PT[0][:, 0:256], func=AF.Identity, scale=KDEC[:, c * 8 + h:c * 8 + h + 1]),
                      r=["PT0", "KDEC"], w=["KTf"])
            for dc in range(2):
                def mm(e, dc=dc):
                    ins = None
                    for c in range(4):
                        ins = e.matmul(PS[4 + dc][:, :], lhsT=KTf[:, c, dc * 128:(dc + 1) * 128], rhs=VF_[:, c, :], start=(c == 0), stop=(c == 3))
                    return ins
                P.add("pe", mm, r=["KTf", "VF"], w=["PS%d" % (4 + dc)])
                P.add("dve", lambda e, dc=dc, h=h: e.scalar_tensor_tensor(out=SINF[:, h * 2 + dc, :], in0=PS[4 + dc][:, :], scalar=COEF[:, h:h + 1], in1=SINF[:, h * 2 + dc, :],
                                                                       op0=ALU.mult, op1=ALU.add), r=["PS%d" % (4 + dc), "COEF", "SINF"], w=["SINF"])
                P.add("dve", lambda e, dc=dc, h=h: e.scalar_tensor_tensor(out=SINB[:, h * 2 + dc, :], in0=PS[4 + dc][:, :], scalar=COEF[:, 8 + h:9 + h], in1=SINB[:, h * 2 + dc, :],
                                                                       op0=ALU.mult, op1=ALU.add), r=["PS%d" % (4 + dc), "COEF", "SINB"], w=["SINB"])
    if stop_after != "pre":
        dma("pool", sinb_s[:, :, :], SINB[:, :, :], r=["SINB"], w=["sinb_s"], sem="sinbst")
    P.barrier()
    SS_ = BIG1[:, 0:16384].bitcast(F32).rearrange("p (c n) -> p c n", c=16)
    SC_ = [BIG1[:, 16384 + i * 2048:16384 + (i + 1) * 2048].rearrange("p (c n) -> p c n", c=4) for i in range(2)]
    FL_ = BIG2[:, 0:16384].bitcast(F32).rearrange("p (c n) -> p c n", c=16)
    for dr in range(2 if stop_after not in ("pre", "A") else 0):
        if dr == 1:
            dma("sp", SS_[:, :, :], sinb_s[:, :, :], r=["sinb_s"], w=["SS0", "SS1", "SS2", "SS3"], sem="sinbl")
        order = list(range(NTILE)) if dr == 0 else list(range(NTILE - 1, -1, -1))
        st_d = Sf_s if dr == 0 else Sb_s
        fb_d = F_s if dr == 0 else B_s
        for t in order:
            for q4 in range(4):
                hh = q4 % 2
                P.add("act", lambda e, hh=hh, q4=q4: e.activation(out=SC_[hh][:, :, :], in_=SS_[:, q4 * 4:q4 * 4 + 4, :], func=AF.Copy),
                      r=["SS%d" % q4], w=["SC%d" % hh])
                dma("pool", st_d[t, :, q4 * 4:q4 * 4 + 4, :], SC_[hh][:, :, :], r=["SC%d" % hh], w=["S_s"], sem="scst%d" % hh)
            for hf in range(2):
                dma("sp", FL_[:, hf * 8:hf * 8 + 8, :], fb_d[t, :, hf * 8:hf * 8 + 8, :], r=["F_s", "B_s"], w=["FL%d" % hf], sem="fl%d" % hf)
            for h in range(H):
                P.add("dve", lambda e, h=h, dr=dr: e.scalar_tensor_tensor(out=SS_[:, h * 2:h * 2 + 2, :], in0=SS_[:, h * 2:h * 2 + 2, :],
                                                                         scalar=SMALL[:, 16 + dr * 8 + h:17 + dr * 8 + h], in1=FL_[:, h * 2:h * 2 + 2, :],
                                                                         op0=ALU.mult, op1=ALU.add),
                      r=["SS%d" % (h // 2), "FL%d" % (h // 4), "SMALL"], w=["SS%d" % (h // 2)])
    P.barrier()
    GATT = BIG1[:, 0:16384].rearrange("p (c n) -> p c n", c=32)
    QB = [BIG2[:, i * 1024:(i + 1) * 1024].rearrange("p (c n) -> p c n", c=2) for i in range(2)]
    KB = [BIG2[:, 2048 + i * 1024:2048 + (i + 1) * 1024].rearrange("p (c n) -> p c n", c=2) for i in range(2)]
    VB = [BIG2[:, 4096 + i * 2048:4096 + (i + 1) * 2048].rearrange("p (c n) -> p c n", c=4) for i in range(2)]
    SGB = [BIG2[:, 8192 + i * 2048:8192 + (i + 1) * 2048].rearrange("p (c n) -> p c n", c=4) for i in range(2)]
    PCB2 = PC[:, 3072:7168].bitcast(BF16)
    SFc = [[BIG2[:, 12288 + c * 1024:12288 + (c + 1) * 1024].rearrange("p (c n) -> p c n", c=2) for c in range(4)],
           [PCB2[:, c * 1024:(c + 1) * 1024].rearrange("p (c n) -> p c n", c=2) for c in range(4)]]
    SBc = [[BIG1[:, 16384 + c * 1024:16384 + (c + 1) * 1024].rearrange("p (c n) -> p c n", c=2) for c in range(4)],
           [PCB2[:, 4096 + c * 1024:4096 + (c + 1) * 1024].rearrange("p (c n) -> p c n", c=2) for c in range(4)]]
    QH = [BIG2[:, 16384:18432].rearrange("p (d c n) -> p d c n", d=2, c=2),
          BIG1[:, 20480:22528].rearrange("p (d c n) -> p d c n", d=2, c=2)]
    KTC = [XN[:, i * 256:(i + 1) * 256] for i in range(2)]
    PB_ = HN[:, 0, 0:256].rearrange("p (i n) -> p i n", i=2)
    GAT = [HN[:, 1 + i, 0:512] for i in range(2)]
    Y1 = TMP[2]

    def prep_loads(t, h, b):
        t0 = t * T
        dma("sp", QB[b][:, :, :], qT_s[h, :, :, t0:t0 + T].rearrange("c p n -> p c n"), r=["qT_s"], w=["QB%d" % b], sem="qb%d" % b)
        dma("sp", KB[b][:, :, :], kT_s[h, :, :, t0:t0 + T].rearrange("c p n -> p c n"), r=["kT_s"], w=["KB%d" % b], sem="kb%d" % b)
        dma("sp", VB[b][:, :, :], v_s[h, t0:t0 + T, :].rearrange("(tb p) n -> p tb n", p=128), r=["v_s"], w=["VB%d" % b], sem="vb%d" % b)
        dma("sp", SGB[b][:, :, :], sg_s[h, t0:t0 + T, :].rearrange("(tb p) n -> p tb n", p=128), r=["sg_s"], w=["SGB%d" % b], sem="sgb%d" % b)
        dma("sp", SFc[b][0][:, :, :], Sf_s[t, :, h * 2:h * 2 + 2, :], r=["S_s"], w=["SF%d_0" % b], sem="sfl%d" % b)
        dma("sp", SBc[b][3][:, :, :], Sb_s[t, :, h * 2:h * 2 + 2, :], r=["S_s"], w=["SB%d_3" % b], sem="sbl%d" % b)
        for dr in range(2):
            P.add("dve", lambda e, b=b, dr=dr, h=h: e.tensor_tensor(
                out=QH[b][:, dr, :, :].rearrange("p c (k i) -> p (c k) i", i=128),
                in0=QB[b][:, :, :].rearrange("p c (k i) -> p (c k) i", i=128),
                in1=QD[:, dr * 8 + h:dr * 8 + h + 1, :].to_broadcast([128, 8, 128]), op=ALU.mult),
                r=["QB%d" % b, "QD"], w=["QH%d" % b])

    def chain_step(t, h, b, step):
        for dr in range(2):
            c = step if dr == 0 else 3 - step
            kt = KTC[(step * 2 + dr) % 2]
            ktk = "KTC%d" % ((step * 2 + dr) % 2)

            def tr(e, b=b, c=c):
                ins = None
                for dc in range(2):
                    ins = e.transpose(out=PT[0][:, dc * 128:(dc + 1) * 128], in_=KB[b][:, dc, c * 128:(c + 1) * 128], identity=IDB[:, :])
                return ins
            P.add("pe", tr, r=["KB%d" % b, "IDB"], w=["PT0"])
            P.add("act", lambda e, kt=kt, dr=dr, h=h: e.activation(out=kt[:, :], in_=PT[0][:, 0:256], func=AF.Identity,
                                                                 scale=DEC[:, 64 + dr * 8 + h:65 + dr * 8 + h]),
                  r=["PT0", "DEC"], w=[ktk])
            src_st = SFc[b][c] if dr == 0 else SBc[b][c]
            dst_st = SFc[b][c + 1] if dr == 0 else SBc[b][c - 1]
            skey = ("SF%d_%d" % (b, c)) if dr == 0 else ("SB%d_%d" % (b, c))
            dkey = ("SF%d_%d" % (b, c + 1)) if dr == 0 else ("SB%d_%d" % (b, c - 1))
            for dc in range(2):
                P.add("pe", lambda e, kt=kt, dc=dc, b=b, c=c: e.matmul(PS[4 + dc][:, :], lhsT=kt[:, dc * 128:(dc + 1) * 128], rhs=VB[b][:, c, :], start=True, stop=True),
                      r=[ktk, "VB%d" % b], w=["PS%d" % (4 + dc)])
                P.add("dve", lambda e, dc=dc, src_st=src_st, dst_st=dst_st, dr=dr, h=h: e.scalar_tensor_tensor(
                    out=dst_st[:, dc, :], in0=src_st[:, dc, :], scalar=SMALL[:, dr * 8 + h:dr * 8 + h + 1], in1=PS[4 + dc][:, :],
                    op0=ALU.mult, op1=ALU.add), r=["PS%d" % (4 + dc), skey, "SMALL"], w=[dkey])

    def out_chunk(t, h, b, c):
        if h % 4 == 0 and c == 0:
            dma("sp", GT[:], gn_gain[0:1, (h // 4) * 2048:(h // 4 + 1) * 2048].partition_broadcast(128), r=(), w=["GT"], sem="gt")
        pi = c % 2

        def sc(e, b=b, c=c):
            ins = None
            for dc in range(2):
                ins = e.matmul(PS[0][:, 0:128], lhsT=KB[b][:, dc, c * 128:(c + 1) * 128], rhs=QB[b][:, dc, c * 128:(c + 1) * 128],
                               start=(dc == 0), stop=(dc == 1))
            return ins
        P.add("pe", sc, r=["KB%d" % b, "QB%d" % b], w=["PS0"])
        P.add("dve", lambda e, pi=pi, h=h: e.tensor_tensor(out=PB_[:, pi, :], in0=PS[0][:, 0:128], in1=DM[:, h, :], op=ALU.mult),
              r=["PS0", "DM"], w=["PB%d" % pi])
        ob = PS[1 + pi]

        def om(e, b=b, c=c, pi=pi, ob=ob):
            e.matmul(ob[:, :], lhsT=PB_[:, pi, :], rhs=VB[b][:, c, :], start=True, stop=False)
            e.matmul(ob[:, :], lhsT=QH[b][:, 0, 0, c * 128:(c + 1) * 128], rhs=SFc[b][c][:, 0, :], start=False, stop=False)
            e.matmul(ob[:, :], lhsT=QH[b][:, 0, 1, c * 128:(c + 1) * 128], rhs=SFc[b][c][:, 1, :], start=False, stop=False)
            e.matmul(ob[:, :], lhsT=QH[b][:, 1, 0, c * 128:(c + 1) * 128], rhs=SBc[b][c][:, 0, :], start=False, stop=False)
            return e.matmul(ob[:, :], lhsT=QH[b][:, 1, 1, c * 128:(c + 1) * 128], rhs=SBc[b][c][:, 1, :], start=False, stop=True)
        P.add("pe", om, r=["PB%d" % pi, "VB%d" % b, "QH%d" % b, "SF%d_%d" % (b, c), "SB%d_%d" % (b, c)], w=["PS%d" % (1 + pi)])
        okey = "PS%d" % (1 + pi)
        P.add("dve", lambda e, ob=ob: e.bn_stats(out=STAT[:, 0:6], in_=ob[:, :]), r=[okey], w=["STAT"])
        P.add("dve", lambda e: e.bn_aggr(out=STAT[:, 8:10], in_=STAT[:, 0:6]), r=["STAT"], w=["STAT2"])
        P.add("act", lambda e: e.activation(out=STAT[:, 10:11], in_=STAT[:, 9:10], func=AF.Sqrt, bias=EPSC[:, 1:2]), r=["STAT2", "EPSC"], w=["STAT3"])
        P.add("dve", lambda e: e.reciprocal(out=STAT[:, 9:10], in_=STAT[:, 10:11]), r=["STAT3", "STAT2"], w=["STAT3"])
        P.add("dve", lambda e, ob=ob, h=h: e.scalar_tensor_tensor(out=Y1[:], in0=ob[:, :], scalar=STAT[:, 8:9], in1=GT[:, (h % 4) * 512:(h % 4 + 1) * 512],
                                                                 op0=ALU.subtract, op1=ALU.mult), r=[okey, "STAT2", "STAT3", "GT"], w=["Y1"])
        P.add("dve", lambda e, b=b, c=c, pi=pi: e.scalar_tensor_tensor(out=GAT[pi], in0=Y1[:], scalar=STAT[:, 9:10], in1=SGB[b][:, c, :],
                                                                    op0=ALU.mult, op1=ALU.mult), r=["Y1", "STAT3", "SGB%d" % b], w=["GAT%d" % pi])

    def out_b(t, h, b, c):
        pi = c % 2

        def trg(e, pi=pi):
            ins = None
            for vc in range(4):
                ins = e.transpose(out=PT[1][:, vc * 128:(vc + 1) * 128], in_=GAT[pi][:, vc * 128:(vc + 1) * 128], identity=IDB[:, :])
            return ins
        P.add("pe", trg, r=["GAT%d" % pi, "IDB"], w=["PT1"])
        P.add("act", lambda e, h=h, c=c: e.activation(out=GATT[:, h * 4:h * 4 + 4, c * 128:(c + 1) * 128],
                                                    in_=PT[1][:, 0:512].rearrange("p (v i) -> p v i", v=4), func=AF.Copy),
              r=["PT1"], w=["GATT"])

    pend_b = [None]
    seq_ = [(t, h) for t in range(NTILE if stop_after not in ("pre", "A", "scan") else 0) for h in range(H)]
    if seq_:
        prep_loads(seq_[0][0], seq_[0][1], 0)
        for st_ in range(3):
            chain_step(seq_[0][0], seq_[0][1], 0, st_)
    for i_, (t, h) in enumerate(seq_):
        b = i_ % 2
        nxt = seq_[i_ + 1] if i_ + 1 < len(seq_) else None
        if nxt is not None:
            prep_loads(nxt[0], nxt[1], 1 - b)
        for c in range(4):
            out_chunk(t, h, b, c)
            if pend_b[0] is not None:
                out_b(*pend_b[0])
            pend_b[0] = (t, h, b, c)
            if nxt is not None and c < 3:
                chain_step(nxt[0], nxt[1], 1 - b, c)
        if h == H - 1:
            out_b(*pend_b[0])
            pend_b[0] = None
        if h == H - 1:
            back_end(GATT, "GATT", 32, 16, wb_ret_out, "wb_ret_out", x_in, "x_in", t * T, xs[0], "xs0", t * T + 1, T)
    P.barrier()

    def ffn_phase(li, src, skey, dst, dkey, tok0=0, ntok=None):
        ntok = NT if ntok is None else ntok
        load_gain_table(ln_ffn[li:li + 1, :], "gt")
        cwv = CW[:, :].rearrange("p (c k) -> p c k", k=4)
        for k in range(3):
            dma("sp", cwv[:, :, k], ffn_conv_w[li][k, :].rearrange("(c p) -> p c", p=128), r=(), w=["CW"], sem="cw", slow=True)
        dma("sp", cwv[:, :, 3], ffn_conv_b[li][0, :].rearrange("(c p) -> p c", p=128), r=(), w=["CW"], sem="cw", slow=True)
        HT = BIG1[:, 0:44 * 512].rearrange("p (c n) -> p c n", c=44)
        CT = [BIG2[:, i * 1024:(i + 1) * 1024].bitcast(F32) for i in range(6)]
        for (s0_, sz) in ffn_tiles(ntok):
            s0 = s0_ + tok0
            ncol = sz + 2
            front_end(src, skey, s0, ncol, True)
            for pb in range(22):
                slot = load_w([(16, 0, 256, wtile(wb_ffn_in[li], 0, 16, pb * 256, 256)),
                               (16, 256, 256, wtile(wb_ffn_in[li], 0, 16, FF + pb * 256, 256))], "wb_ffn_in%d" % li)
                for bk in range(4):
                    def mm(e, slot=slot, bk=bk, ncol=ncol):
                        ins = None
                        for kc in range(16):
                            ins = e.matmul(PS[bk][:, 0:ncol], lhsT=slot.ap(kc, bk * 128, (bk + 1) * 128), rhs=HN[:, kc, 0:ncol],
                                           start=(kc == 0), stop=(kc == 15))
                        return ins
                    P.add("pe", mm, r=slot.keys + ["HN"], w=["PS%d" % bk])
                for i in range(2):
                    outs = []
                    for gu in range(2):
                        bk = gu * 2 + i
                        ch = (gu * 44 + pb * 2 + i)
                        ps = PS[bk]
                        pk = "PS%d" % bk
                        ct = CT[gu * 3]
                        ck = "CT%d" % (gu * 3)
                        P.add("act", lambda e, ps=ps, ct=ct, ch=ch, sz=sz: e.activation(out=ct[:, 0:sz], in_=ps[:, 1:sz + 1], func=AF.Identity,
                                                                                     scale=CW[:, ch * 4 + 1:ch * 4 + 2], bias=CW[:, ch * 4 + 3:ch * 4 + 4]),
                              r=[pk, "CW"], w=[ck])
                        ct2 = CT[gu * 3 + 1]
                        ck2 = "CT%d" % (gu * 3 + 1)
                        P.add("dve", lambda e, ps=ps, ct=ct, ct2=ct2, ch=ch, sz=sz: e.scalar_tensor_tensor(out=ct2[:, 0:sz], in0=ps[:, 0:sz], scalar=CW[:, ch * 4:ch * 4 + 1],
                                                                                                      in1=ct[:, 0:sz], op0=ALU.mult, op1=ALU.add),
                              r=[pk, "CW", ck], w=[ck2])
                        ct3 = CT[gu * 3 + 2]
                        ck3 = "CT%d" % (gu * 3 + 2)
                        P.add("dve", lambda e, ps=ps, ct2=ct2, ct3=ct3, ch=ch, sz=sz: e.scalar_tensor_tensor(out=ct3[:, 0:sz], in0=ps[:, 2:sz + 2], scalar=CW[:, ch * 4 + 2:ch * 4 + 3],
                                                                                                        in1=ct2[:, 0:sz], op0=ALU.mult, op1=ALU.add),
                              r=[pk, "CW", ck2], w=[ck3])
                        outs.append((ct3, ck3))
                    (gt_, gk), (ut_, uk) = outs
                    P.add("act", lambda e, gt_=gt_, sz=sz: e.activation(out=CT[0][:, 0:sz], in_=gt_[:, 0:sz], func=AF.Silu), r=[gk], w=["CT0"])
                    P.add("dve", lambda e, ut_=ut_, pb=pb, i=i, sz=sz: e.tensor_tensor(out=HT[:, pb * 2 + i, 0:sz], in0=CT[0][:, 0:sz], in1=ut_[:, 0:sz], op=ALU.mult),
                          r=["CT0", uk], w=["HT"])
            back_end(HT, "HT", 44, 11, wb_ffn_out[li], "wb_ffn_out%d" % li, src, skey, s0 + 1, dst, dkey, s0 + 1, sz)
        P.barrier()

    def sgu_phase(src, skey, dst, dkey):
        TS = 256
        load_gain_table(ln_mix[1:2, :], "gt")
        MX = BIG1[:, 0:48 * TS].rearrange("p (c n) -> p c n", c=48)
        VV = BIG2[:, 0:2 * E].rearrange("p (c n) -> p c n", c=2)
        CC = PC[:, 0:3072].bitcast(BF16).rearrange("p (c n) -> p c n", c=48)
        SEL = PC[0:24, 3072:4608].bitcast(BF16).rearrange("p (c n) -> p c n", c=24)
        LNG = PC[:, 4608:4656]
        LNB = PC[:, 4656:4704]
        BIU = PC[:, 4704:4752]
        BSR = PC[:, 4752:5776].rearrange("p (g i) -> p g i", g=8)
        WST = PC[:, 5776:6288].bitcast(BF16).rearrange("p (g i) -> p g i", g=8)
        BROWF = PC[0:24, 6288:6800]
        BROW = PC[0:24, 6800:7056].bitcast(BF16)
        dma("sp", LNG, sgu_ln_g[0, :].rearrange("(c p) -> p c", p=128), r=(), w=["SGP"], sem="sgp", slow=True)
        dma("sp", LNB, sgu_ln_b[0, :].rearrange("(c p) -> p c", p=128), r=(), w=["SGP"], sem="sgp", slow=True)
        dma("sp", BIU, sgu_b_in[0, 0:E].rearrange("(c p) -> p c", p=128), r=(), w=["SGP"], sem="sgp", slow=True)
        dma("sp", PC[:, 4752:5776], sgu_b_s.rearrange("(o g) i -> o (g i)", o=1).partition_broadcast(128), r=(), w=["SGP"], sem="sgp")
        dma("sp", BROWF, sgu_b_in[0, :].rearrange("(r n) -> r n", n=512), r=(), w=["BROWF"], sem="brow")
        dma("sp", SEL, sel_d[:, :, :], r=(), w=["SEL"], sem="sel")
        P.add("dve", lambda e: e.tensor_copy(out=BROW, in_=BROWF), r=["BROWF"], w=["BROW"])
        P.add("dve", lambda e: e.memset(XN[:, 128:256], 1.0), r=(), w=["XN1"])
        for g in range(G):
            dma("sp", TMP[0][:, 0:128], sgu_w_s[g, :, :], r=(), w=["TMP0"], sem="wsl")
            P.add("dve", lambda e: e.tensor_copy(out=XN[:, 0:128], in_=TMP[0][:, 0:128]), r=["TMP0"], w=["XN"])
            P.add("pe", lambda e: e.transpose(out=PT[0][:, 0:128], in_=XN[:, 0:128], identity=IDB[:, :]), r=["XN", "IDB"], w=["PT0"])
            P.add("act", lambda e, g=g: e.activation(out=WST[:, g, :], in_=PT[0][:, 0:128], func=AF.Copy), r=["PT0"], w=["WST"])
            P.add("pe", lambda e, g=g: e.matmul(PS[0][:, 0:128], lhsT=XN[:, 128:256], rhs=WST[:, g, :], start=True, stop=True), r=["XN1", "WST"], w=["PS0"])
            for k in range(6):
                fc = g * 6 + k
                P.add("dve", lambda e, fc=fc, g=g: e.scalar_tensor_tensor(out=CC[:, fc, :], in0=PS[0][:, 0:128], scalar=LNB[:, fc:fc + 1], in1=BSR[:, g, :],
                                                                       op0=ALU.mult, op1=ALU.add), r=["PS0", "SGP"], w=["CC"])
        for t in range(NT // TS):
            t0 = t * TS
            front_end(src, skey, t0 + 1, TS, False)
            for cb in range(12):
                slot = load_w([(16, 0, 512, wtile(wb_sgu_in, 0, 16, E + cb * 512, 512))], "wb_sgu_in")
                for tb in range(2):
                    bk = (cb % 2) * 2 + tb

                    def mm(e, slot=slot, tb=tb, cb=cb, bk=bk):
                        for kc in range(16):
                            e.matmul(PS[bk][:, :], lhsT=HN[:, kc, tb * 128:(tb + 1) * 128], rhs=slot.ap(kc, 0, 512), start=(kc == 0), stop=False)
                        return e.matmul(PS[bk][:, :], lhsT=SEL[:, 12 + cb, :], rhs=BROW, start=False, stop=True)
                    P.add("pe", mm, r=slot.keys + ["HN", "SEL", "BROW"], w=["PS%d" % bk])
                    P.add("act", lambda e, tb=tb, cb=cb, bk=bk: e.activation(out=VV[:, tb, cb * 512:(cb + 1) * 512], in_=PS[bk][:, :], func=AF.Gelu),
                          r=["PS%d" % bk], w=["VV%d" % tb])
                    P.add("dve", lambda e, tb=tb, cb=cb: e.bn_stats(out=STAT[:, (tb * 12 + cb) * 6:(tb * 12 + cb + 1) * 6], in_=VV[:, tb, cb * 512:(cb + 1) * 512]),
                          r=["VV%d" % tb], w=["STATV%d" % tb])
            for tb in range(2):
                mv = SMALL[:, 40 + tb * 2:42 + tb * 2]
                P.add("dve", lambda e, tb=tb, mv=mv: e.bn_aggr(out=mv, in_=STAT[:, tb * 72:(tb + 1) * 72]), r=["STATV%d" % tb], w=["MV%d" % tb])
                P.add("act", lambda e, mv=mv: e.activation(out=mv[:, 1:2], in_=mv[:, 1:2], func=AF.Sqrt, bias=EPSC[:, 1:2]), r=["MV%d" % tb, "EPSC"], w=["MVb%d" % tb])
                P.add("dve", lambda e, mv=mv: e.reciprocal(out=mv[:, 1:2], in_=mv[:, 1:2]), r=["MVb%d" % tb], w=["MVb%d" % tb])
                P.add("dve", lambda e, tb=tb, mv=mv: e.tensor_scalar(out=VV[:, tb, :], in0=VV[:, tb, :], scalar1=mv[:, 0:1], scalar2=mv[:, 1:2], op0=ALU.subtract, op1=ALU.mult),
                      r=["MV%d" % tb, "MVb%d" % tb, "VV%d" % tb], w=["VV%d" % tb])
            for fc in range(48):
                g = fc // 6
                bk = 4 + (fc % 2)

                def mm(e, fc=fc, g=g, bk=bk):
                    ins = None
                    for c in range(2):
                        ins = e.matmul(PS[bk][:, c * 128:(c + 1) * 128], lhsT=VV[:, c, fc * 128:(fc + 1) * 128], rhs=WST[:, g, :], start=True, stop=True)
                    return ins
                P.add("pe", mm, r=["VV0", "VV1", "WST"], w=["PS%d" % bk])
                P.add("dve", lambda e, fc=fc, bk=bk: e.scalar_tensor_tensor(out=MX[:, fc, :].rearrange("p (c i) -> p c i", c=2),
                                                                         in0=PS[bk][:, 0:256].rearrange("p (c i) -> p c i", c=2), scalar=LNG[:, fc:fc + 1],
                                                                         in1=CC[:, fc:fc + 1, :].to_broadcast([128, 2, 128]), op0=ALU.mult, op1=ALU.add),
                      r=["PS%d" % bk, "CC", "SGP"], w=["MX"])
            for cb in range(12):
                slot = load_w([(16, 0, 512, wtile(wb_sgu_in, 0, 16, cb * 512, 512))], "wb_sgu_in")
                for bk in range(4):
                    fc = cb * 4 + bk

                    def mm(e, slot=slot, bk=bk):
                        ins = None
                        for kc in range(16):
                            ins = e.matmul(PS[bk][:, 0:TS], lhsT=slot.ap(kc, bk * 128, (bk + 1) * 128), rhs=HN[:, kc, 0:TS], start=(kc == 0), stop=(kc == 15))
                        return ins
                    P.add("pe", mm, r=slot.keys + ["HN"], w=["PS%d" % bk])
                    tm = TMP[bk % 2]
                    P.add("act", lambda e, bk=bk, fc=fc, tm=tm: e.activation(out=tm[:, 0:TS], in_=PS[bk][:, 0:TS], func=AF.Gelu, bias=BIU[:, fc:fc + 1]),
                          r=["PS%d" % bk, "SGP"], w=["TMP%d" % (bk % 2)])
                    P.add("dve", lambda e, fc=fc, tm=tm: e.tensor_tensor(out=MX[:, fc, :], in0=tm[:, 0:TS], in1=MX[:, fc, :], op=ALU.mult),
                          r=["TMP%d" % (bk % 2), "MX"], w=["MX"])
            back_end(MX, "MX", 48, 16, wb_sgu_out, "wb_sgu_out", src, skey, t0 + 1, dst, dkey, t0 + 1, TS)
        P.barrier()

    def final_phase(src, skey, row_off):
        row_off = row_off + EXT
        load_gain_table(ln_final[0:1, :], "gt")
        n = 0
        for (r0, nb, _) in _blocks(0, NOWN):
            i = n % 2
            n += 1
            xin = XIN[i]
            dma("sp", xin[0:nb, :], src[row_off + r0:row_off + r0 + nb, :], r=[skey], w=["XIN%d" % i], sem="xin%d" % i)
            ss = SMALL[:, 32 + i:33 + i]
            P.add("act", lambda e, xin=xin, ss=ss: e.activation(out=XN[:, :], in_=xin[:, :], func=AF.Square, accum_out=ss),
                  r=["XIN%d" % i], w=["XN", "SS%d" % i])
            P.add("act", lambda e, ss=ss: e.activation(out=ss, in_=ss, func=AF.Sqrt, scale=1.0 / D, bias=EPSC[:, 0:1]), r=["SS%d" % i, "EPSC"], w=["SS%d" % i])
            P.add("dve", lambda e, ss=ss: e.reciprocal(out=ss, in_=ss), r=["SS%d" % i], w=["SS%d" % i])
            P.add("dve", lambda e, xin=xin, ss=ss: e.scalar_tensor_tensor(out=xin[:, :], in0=xin[:, :], scalar=ss, in1=GT[:, :], op0=ALU.mult, op1=ALU.mult),
                  r=["XIN%d" % i, "SS%d" % i, "GT"], w=["XIN%d" % i])
            dma("pool", y_out[r0:r0 + nb, :], xin[0:nb, :], r=["XIN%d" % i], w=["y_out"], sem="yst%d" % i)

    if stop_after in ("pre", "A", "scan"):
        final_phase(x_in, "x_in", 0)
    elif stop_after == "mix0":
        final_phase(xs[0], "xs0", 1)
    else:
        ffn_phase(0, xs[0], "xs0", xs[1], "xs1")
        if stop_after == "ffn0":
            final_phase(xs[1], "xs1", 1)
        else:
            sgu_phase(xs[1], "xs1", xs[2], "xs2")
            if stop_after == "sgu":
                final_phase(xs[2], "xs2", 1)
            else:
                ffn_phase(1, xs[2], "xs2", xs[3], "xs3", EXT, NOWN)
                final_phase(xs[3], "xs3", 1)
    P.add("sp", None, r=["y_out"], w=())
    P.barrier()

    P.finalize(nc, sem_alloc)
    blk = es.enter_context(nc.Block())

    @blk.tensor
    def _(e):
        P.emit("pe", e)

    @blk.scalar
    def _(e):
        P.emit("act", e)

    @blk.vector
    def _(e):
        P.emit("dve", e)

    @blk.gpsimd
    def _(e):
        P.emit("pool", e)

    @blk.sync
    def _(e):
        P.emit("sp", e)

    es.close()
    return nc


_WNAMES = ["ln_mix", "ln_ffn", "ret_w_in", "ret_gn_gain", "ret_w_out", "sgu_w_in", "sgu_b_in", "sgu_ln_g", "sgu_ln_b",
           "sgu_w_s", "sgu_b_s", "sgu_w_out", "ln_final"]


def shared_inputs(inp):
    f = lambda a: np.ascontiguousarray(np.asarray(a, dtype=np.float32))
    m = {
        "ln_mix": f(inp["ln_mix"]), "ln_ffn": f(inp["ln_ffn"]),
        "ret_w_in": f(inp["ret_w_in"][0]), "lgf": f(inp["ret_log_decay_fwd"]), "lgb": f(inp["ret_log_decay_bwd"]),
        "gn_gain": f(inp["ret_gn_gain"]), "ret_w_out": f(inp["ret_w_out"][0]),
        "sgu_w_in": f(inp["sgu_w_in"][0]), "sgu_b_in": f(inp["sgu_b_in"]), "sgu_ln_g": f(inp["sgu_ln_g"]),
        "sgu_ln_b": f(inp["sgu_ln_b"]), "sgu_w_s": f(inp["sgu_w_s"][0]), "sgu_b_s": f(inp["sgu_b_s"][0]),
        "sgu_w_out": f(inp["sgu_w_out"][0]), "ln_final": f(inp["ln_final"]).reshape(1, D),
    }
    for i in range(2):
        m["ffn_w_in%d" % i] = f(inp["ffn_w_in"][i])
        m["ffn_conv_w%d" % i] = f(inp["ffn_conv_w"][i])
        m["ffn_conv_b%d" % i] = f(inp["ffn_conv_b"][i]).reshape(1, 2 * FF)
        m["ffn_w_out%d" % i] = f(inp["ffn_w_out"][i])
    m.update(make_consts())
    return m


def core_inputs(seq, a, NOWN, NF):
    L = seq.shape[0]
    NT = NOWN + 2 * EXT
    es_, ee_ = a - EXT, a + NOWN + EXT

    def gather(t0, n):
        out = np.zeros((n, D), np.float32)
        lo, hi = max(t0, 0), min(t0 + n, L)
        if hi > lo:
            out[lo - t0:hi - t0] = seq[lo:hi]
        return out
    d = {}
    d["x"] = gather(es_, NT)
    tk = np.arange(es_ - 1, es_ - 1 + NT + 2)
    d["tokmask"] = ((tk >= 0) & (tk < L)).astype(np.float32).reshape(NT + 2, 1)
    d["cos"], d["sin"] = rope_tables(np.clip(np.arange(es_, ee_), 0, None))
    slots = []
    i = 0
    while es_ - T * i > 0:
        slots.append((es_ - T * (i + 1), True, T * i))
        i += 1
    i = 0
    while ee_ + T * i < L:
        slots.append((ee_ + T * i, False, T * i))
        i += 1
    assert len(slots) <= NF, (len(slots), NF)
    xf = np.zeros((NF * T, D), np.float32)
    pos = np.zeros((NF * T,), np.float32)
    ftab = np.zeros((128, NF * 12), np.float32)
    p = np.arange(128, dtype=np.float32)
    for k, (s0, before, dist) in enumerate(slots):
        xf[k * T:(k + 1) * T] = gather(s0, T)
        pos[k * T:(k + 1) * T] = np.clip(np.arange(s0, s0 + T), 0, None)
        for c in range(4):
            if before:
                ftab[:, k * 12 + c] = 511.0 - (c * 128 + p)
            else:
                ftab[:, k * 12 + 4 + c] = c * 128 + p
        if before:
            ftab[:, k * 12 + 8] = dist
            ftab[:, k * 12 + 10] = 1.0
        else:
            ftab[:, k * 12 + 9] = dist
            ftab[:, k * 12 + 11] = 1.0
    d["xf"] = xf
    d["cosf"], d["sinf"] = rope_tables(pos)
    d["ftab"] = ftab
    return d


def run_sequences(inp, seqs, NOWN, NF, stop_after="all"):
    nc = build(NOWN, NF, stop_after)
    sh = shared_inputs(inp)
    plan = []
    for si, sq in enumerate(seqs):
        for a in range(0, sq.shape[0], NOWN):
            plan.append((si, a))
    in_maps = []
    for (si, a) in plan:
        d = dict(sh)
        d.update(core_inputs(seqs[si], a, NOWN, NF))
        in_maps.append(d)
    res = run_bass_kernel_spmd(nc, in_maps, core_ids=list(range(len(plan))))
    outs = [np.zeros((sq.shape[0], D), np.float32) for sq in seqs]
    for c, (si, a) in enumerate(plan):
        outs[si][a:a + NOWN] = res.results[c]["y"]
    return outs


def kernel(**inputs):
    xp = np.asarray(inputs["x_prompt"], dtype=np.float32)
    xsm = np.asarray(inputs["x_sample"], dtype=np.float32)
    seqs = [xp[i] for i in range(xp.shape[0])] + [xsm[i] for i in range(xsm.shape[0])]
    total = sum(sq.shape[0] for sq in seqs)
    NOWN = total // N_CORES
    maxL = max(sq.shape[0] for sq in seqs)
    NF = max(1, -(-(maxL - NOWN - EXT) // T))
    outs = run_sequences(inputs, seqs, NOWN, NF)
    y_prompt = np.stack(outs[:xp.shape[0]], 0).astype(np.float32)
    y_sample = np.stack(outs[xp.shape[0]:], 0).astype(np.float32)
    return (y_prompt, y_sample)
```
